# Optimizing a Trainium2 kernel written in Bass

```python
import jax, jax.numpy as jnp
from jax import lax
import numpy as np

D_MODEL = 2048
BATCH = 2
SEQ = 8192
DEPTH = 4

GRID_W = 64
CTX_LEN = 256
N_MIXERS = 2
N_CONV_LAYERS = (DEPTH + 1) // N_MIXERS
N_ATTN_LAYERS = DEPTH // N_MIXERS
CONV_WIDTH = 3
HEAD_DIM = 64
N_HEADS = D_MODEL // HEAD_DIM
N_KV_HEADS = 4
GROUP = N_HEADS // N_KV_HEADS
QKV_WIDTH = (N_HEADS + 2 * N_KV_HEADS) * HEAD_DIM
WINDOW = 128
BLOCK = 128
ROPE_BASE = 10000.0
D_FF = -(-8 * D_MODEL // (3 * 256)) * 256
N_MOD = 6
EPS = 1e-6
NEG_INF = -1e30

kernel_name = 'hybrid_conv_swa_dit_trunk'


def rmsnorm(x, g):
    xf = x.astype(jnp.float32)
    y = xf * lax.rsqrt(jnp.mean(xf * xf, axis=-1, keepdims=True) + EPS)
    return (y * g.astype(jnp.float32)).astype(x.dtype)


def modulate(h, shift, scale):
    return h * (1 + scale) + shift


def axial_rope_tables(rows, cols, dtype):
    n_freq = HEAD_DIM // 4
    inv = ROPE_BASE ** (-jnp.arange(n_freq, dtype=jnp.float32) / n_freq)
    ang_r = rows.astype(jnp.float32)[:, None] * inv
    ang_c = cols.astype(jnp.float32)[:, None] * inv
    return (jnp.cos(ang_r).astype(dtype), jnp.sin(ang_r).astype(dtype),
            jnp.cos(ang_c).astype(dtype), jnp.sin(ang_c).astype(dtype))


def rope_half(x, cos, sin):
    x1, x2 = jnp.split(x, 2, axis=-1)
    c = cos[:, None, :]
    s = sin[:, None, :]
    return jnp.concatenate([x1 * c - x2 * s, x2 * c + x1 * s], axis=-1)


def apply_axial_rope(x, rope):
    cos_r, sin_r, cos_c, sin_c = rope
    x_row, x_col = jnp.split(x, 2, axis=-1)
    return jnp.concatenate([rope_half(x_row, cos_r, sin_r), rope_half(x_col, cos_c, sin_c)], axis=-1)


def depthwise_conv3(u, w):
    return lax.conv_general_dilated(u, w[:, None, :], window_strides=(1,),
                                    padding=[(CONV_WIDTH // 2, CONV_WIDTH // 2)],
                                    dimension_numbers=('NWC', 'WIO', 'NWC'),
                                    feature_group_count=u.shape[-1])


def short_conv_mixer(h, w_in, k, w_out):
    b_gate, c_gate, v = jnp.split(h @ w_in, 3, axis=-1)
    return (b_gate * depthwise_conv3(c_gate * v, k)) @ w_out


def swiglu(h, w_gate, w_up, w_down):
    return (jax.nn.silu(h @ w_gate) * (h @ w_up)) @ w_down


def split_qkv(qkv):
    B, L, _ = qkv.shape
    q = qkv[..., :N_HEADS * HEAD_DIM].reshape(B, L, N_HEADS, HEAD_DIM)
    kv = qkv[..., N_HEADS * HEAD_DIM:].reshape(B, L, 2, N_KV_HEADS, HEAD_DIM)
    return q, kv[:, :, 0], kv[:, :, 1]


def latent_window_attention(q, k, v, k_ctx, v_ctx, sink):
    B, S = q.shape[:2]
    C = k_ctx.shape[1]
    nb = S // BLOCK
    scale = HEAD_DIM ** -0.5
    qb = q.reshape(B, nb, BLOCK, N_KV_HEADS, GROUP, HEAD_DIM).transpose(1, 0, 2, 3, 4, 5)

    def band(t):
        tp = jnp.pad(t, ((0, 0), (BLOCK, BLOCK), (0, 0), (0, 0)))
        tb = tp.reshape(B, nb + 2, BLOCK, N_KV_HEADS, HEAD_DIM)
        bd = jnp.concatenate([tb[:, :-2], tb[:, 1:-1], tb[:, 2:]], axis=2)
        return bd.transpose(1, 0, 2, 3, 4)

    kb, vb = band(k), band(v)
    sink_b = jnp.broadcast_to(sink.astype(jnp.float32).reshape(1, N_KV_HEADS, GROUP, 1, 1),
                              (B, N_KV_HEADS, GROUP, BLOCK, 1))
    r = jnp.arange(BLOCK)[:, None]
    s = jnp.arange(3 * BLOCK)[None, :]

    def one_block(args):
        blk, qi, ki, vi = args
        j = (blk - 1) * BLOCK + s
        valid = (jnp.abs(BLOCK + r - s) <= WINDOW) & (j >= 0) & (j < S)
        s_loc = jnp.einsum('bqkgd,bskd->bkgqs', qi, ki, preferred_element_type=jnp.float32) * scale
        s_loc = jnp.where(valid, s_loc, NEG_INF)
        s_ctx = jnp.einsum('bqkgd,bckd->bkgqc', qi, k_ctx, preferred_element_type=jnp.float32) * scale
        p = jax.nn.softmax(jnp.concatenate([s_loc, s_ctx, sink_b], axis=-1), axis=-1)
        p_loc = p[..., :3 * BLOCK].astype(vi.dtype)
        p_ctx = p[..., 3 * BLOCK:3 * BLOCK + C].astype(vi.dtype)
        o = (jnp.einsum('bkgqs,bskd->bqkgd', p_loc, vi)
             + jnp.einsum('bkgqc,bckd->bqkgd', p_ctx, v_ctx))
        return o.reshape(B, BLOCK, N_HEADS * HEAD_DIM)

    out = lax.map(one_block, (jnp.arange(nb), qb, kb, vb))
    return out.transpose(1, 0, 2, 3).reshape(B, S, N_HEADS * HEAD_DIM)


def context_attention(q_ctx, k_ctx, v_ctx, sink):
    B, C = q_ctx.shape[:2]
    scale = HEAD_DIM ** -0.5
    qg = q_ctx.reshape(B, C, N_KV_HEADS, GROUP, HEAD_DIM)
    sc = jnp.einsum('bqkgd,bckd->bkgqc', qg, k_ctx, preferred_element_type=jnp.float32) * scale
    sink_b = jnp.broadcast_to(sink.astype(jnp.float32).reshape(1, N_KV_HEADS, GROUP, 1, 1),
                              (B, N_KV_HEADS, GROUP, C, 1))
    p = jax.nn.softmax(jnp.concatenate([sc, sink_b], axis=-1), axis=-1)[..., :C].astype(v_ctx.dtype)
    o = jnp.einsum('bkgqc,bckd->bqkgd', p, v_ctx)
    return o.reshape(B, C, N_HEADS * HEAD_DIM)


def attention_mixer(h, h_ctx, w_qkv, w_o, sink, rope, need_ctx_out):
    q, k, v = split_qkv(h @ w_qkv)
    q = apply_axial_rope(q, rope)
    k = apply_axial_rope(k, rope)
    q_ctx, k_ctx, v_ctx = split_qkv(h_ctx @ w_qkv)
    y = latent_window_attention(q, k, v, k_ctx, v_ctx, sink) @ w_o
    y_ctx = context_attention(q_ctx, k_ctx, v_ctx, sink) @ w_o if need_ctx_out else None
    return y, y_ctx


def setup_inputs(seed: int = 0) -> dict:
    key = jax.random.key(seed)
    ks = jax.random.split(key, 16)
    f32 = jnp.float32
    D = D_MODEL

    def dense(k, shape, fan_in, gain=1.0):
        return jax.random.normal(k, shape, f32) * (gain * fan_in ** -0.5)

    return {
        'x': jax.random.normal(ks[0], (BATCH, SEQ, D), f32),
        'c': jax.random.normal(ks[1], (BATCH, D), f32),
        'ctx': jax.random.normal(ks[2], (BATCH, CTX_LEN, D), f32),
        'c_ctx': jax.random.normal(ks[3], (D,), f32),
        'w_mod': dense(ks[4], (DEPTH, D, N_MOD * D), D, 0.5),
        'b_mod': 0.02 * jax.random.normal(ks[5], (DEPTH, N_MOD * D), f32),
        'norm_g': 1.0 + 0.05 * jax.random.normal(ks[6], (DEPTH, 4, D), f32),
        'conv_w_in': dense(ks[7], (N_CONV_LAYERS, D, 3 * D), D),
        'conv_k': dense(ks[8], (N_CONV_LAYERS, CONV_WIDTH, D), CONV_WIDTH),
        'conv_w_out': dense(ks[9], (N_CONV_LAYERS, D, D), D),
        'attn_w_qkv': dense(ks[10], (N_ATTN_LAYERS, D, QKV_WIDTH), D),
        'attn_w_o': dense(ks[11], (N_ATTN_LAYERS, N_HEADS * HEAD_DIM, D), N_HEADS * HEAD_DIM),
        'attn_sink': 0.5 * jax.random.normal(ks[12], (N_ATTN_LAYERS, N_HEADS), f32),
        'ffn_w_gate': dense(ks[13], (DEPTH, D, D_FF), D),
        'ffn_w_up': dense(ks[14], (DEPTH, D, D_FF), D),
        'ffn_w_down': dense(ks[15], (DEPTH, D_FF, D), D_FF),
    }


def reference(x, c, ctx, c_ctx, w_mod, b_mod, norm_g, conv_w_in, conv_k, conv_w_out,
              attn_w_qkv, attn_w_o, attn_sink, ffn_w_gate, ffn_w_up, ffn_w_down):
    B, S, D = x.shape
    ROWS = S // GRID_W
    rows = jnp.repeat(jnp.arange(ROWS), GRID_W)
    cols = jnp.tile(jnp.arange(GRID_W), ROWS)
    rope = axial_rope_tables(rows, cols, x.dtype)
    silu_c = jax.nn.silu(c)
    silu_c_ctx = jax.nn.silu(c_ctx)

    for i in range(DEPTH):
        last = i == DEPTH - 1
        mod = (silu_c @ w_mod[i] + b_mod[i])[:, None, :]
        mod_ctx = (silu_c_ctx @ w_mod[i] + b_mod[i])[None, None, :]
        sh1, sc1, g1, sh2, sc2, g2 = jnp.split(mod, N_MOD, axis=-1)
        csh1, csc1, cg1, csh2, csc2, cg2 = jnp.split(mod_ctx, N_MOD, axis=-1)

        h = modulate(rmsnorm(x, norm_g[i, 0]), sh1, sc1)
        h_ctx = modulate(rmsnorm(ctx, norm_g[i, 0]), csh1, csc1)
        j = i // N_MIXERS
        if i % N_MIXERS == 0:
            y = short_conv_mixer(h, conv_w_in[j], conv_k[j], conv_w_out[j])
            y_ctx = short_conv_mixer(h_ctx, conv_w_in[j], conv_k[j], conv_w_out[j]) if not last else None
        else:
            y, y_ctx = attention_mixer(h, h_ctx, attn_w_qkv[j], attn_w_o[j], attn_sink[j], rope,
                                       need_ctx_out=not last)
        x = x + g1 * rmsnorm(y, norm_g[i, 1])

        h = modulate(rmsnorm(x, norm_g[i, 2]), sh2, sc2)
        x = x + g2 * rmsnorm(swiglu(h, ffn_w_gate[i], ffn_w_up[i], ffn_w_down[i]), norm_g[i, 3])

        if not last:
            ctx = ctx + cg1 * rmsnorm(y_ctx, norm_g[i, 1])
            h_ctx = modulate(rmsnorm(ctx, norm_g[i, 2]), csh2, csc2)
            ctx = ctx + cg2 * rmsnorm(swiglu(h_ctx, ffn_w_gate[i], ffn_w_up[i], ffn_w_down[i]), norm_g[i, 3])
    return x
```

```python
import numpy as np
from contextlib import ExitStack
import concourse.bass as bass
import concourse.mybir as mybir
from concourse.bass_utils import run_bass_kernel_spmd

F32 = mybir.dt.float32
BF16 = mybir.dt.bfloat16
ALU = mybir.AluOpType
AF = mybir.ActivationFunctionType

D = 2048
NK = 16
S = 8192
B = 2
CTX = 256
DFF = 5632
NF = 44
HALO = 385
EPS = 1e-6
NEGB = -30000.0
SLOT = 4096
NSLOT = 5


class Cfg:
    def __init__(self, nc_cores=4, layers=(0, 1, 2, 3), tiles=None, dbg=False):
        self.NC = nc_cores
        self.layers = tuple(layers)
        self.per = B * S // nc_cores
        self.NTOK = self.per + 2 * HALO
        self.nblk = (self.NTOK - 2) // 128
        self.tiles = tiles
        self.dbg = dbg

    def lrange(self, i):
        lo = 1 + 128 * i
        hi = self.NTOK - 1 - 128 * i
        return lo, hi


class Tracker:
    def __init__(self, nc, stack):
        self.nc = nc
        self.engs = {"pe": nc.tensor, "act": nc.scalar, "dve": nc.vector,
                     "pool": nc.gpsimd, "sp": nc.sync}
        self.sem = {}
        self.cnt = {}
        self.stack = stack
        for e in self.engs:
            self.sem[e] = stack.enter_context(nc.semaphore("s_" + e))
            self.cnt[e] = 0
        self.dsem = {}
        self.dcnt = {}
        self.waited = {e: {} for e in self.engs}
        self.lastw = {}
        self.readers = {}

    def _deps(self, reads, writes):
        deps = {}

        def add(tok):
            if tok is None:
                return
            name, sem, val = tok
            if name not in deps or deps[name][1] < val:
                deps[name] = (sem, val)
        for k in reads:
            add(self.lastw.get(k))
        for k in writes:
            add(self.lastw.get(k))
            for r in self.readers.get(k, ()):
                add(r)
        return deps

    def _emit_waits(self, eng, deps):
        e = self.engs[eng]
        w = self.waited[eng]
        for name, (sem, val) in deps.items():
            if eng == "pe" and name == "pe":
                continue
            if w.get(name, 0) >= val:
                continue
            e.wait_ge(sem, val)
            w[name] = val

    def _commit(self, tok, reads, writes):
        for k in writes:
            self.lastw[k] = tok
            self.readers[k] = []
        for k in reads:
            self.readers.setdefault(k, []).append(tok)

    def op(self, eng, fn, reads=(), writes=()):
        self._emit_waits(eng, self._deps(reads, writes))
        ins = fn(self.engs[eng])
        self.cnt[eng] += 1
        ins.then_inc(self.sem[eng], 1)
        tok = (eng, self.sem[eng], self.cnt[eng])
        self._commit(tok, reads, writes)
        return tok

    def dma(self, queue, slot, fn, reads=(), writes=()):
        if slot not in self.dsem:
            self.dsem[slot] = self.stack.enter_context(self.nc.semaphore("d_" + slot))
            self.dcnt[slot] = 0
        self._emit_waits(queue, self._deps(reads, writes))
        inss = fn(self.engs[queue])
        if not isinstance(inss, (list, tuple)):
            inss = [inss]
        for ins in inss:
            ins.then_inc(self.dsem[slot], 16)
            self.dcnt[slot] += 16
        tok = ("d_" + slot, self.dsem[slot], self.dcnt[slot])
        self._commit(tok, reads, writes)
        return tok

    def retire(self, keys):
        toks = []
        for k in keys:
            if self.lastw.get(k) is not None:
                toks.append(self.lastw[k])
            toks.extend(self.readers.get(k, ()))
        return toks

    def inherit(self, keys, toks):
        for k in keys:
            self.readers.setdefault(k, []).extend(toks)


class Builder:
    def __init__(self, cfg):
        self.cfg = cfg
        self.nc = bass.Bass("TRN2", target_bir_lowering=False)

    def declare(self):
        nc, cfg = self.nc, self.cfg
        NT = cfg.NTOK
        I = lambda name, shape: nc.dram_tensor(name, list(shape), F32, kind="ExternalInput").ap()
        self.xT = I("xT", [D, NT])
        self.ctxT = I("ctxT", [D, CTX])
        self.cT = I("cT", [128, NK * 2])
        self.ropeC = I("ropeC", [128, NT])
        self.ropeS = I("ropeS", [128, NT])
        self.kbias = I("kbias", [128, cfg.nblk])
        self.vmask = I("vmask", [128, NT])
        self.consts = I("consts", [128, 384])
        self.W = {}
        for i in cfg.layers:
            w = {}
            w["mod"] = I(f"wmod{i}", [48, 128, SLOT])
            w["bm"] = I(f"bm{i}", [128, 96])
            w["ng"] = I(f"ng{i}", [128, 64])
            if i % 2 == 0:
                w["win"] = I(f"win{i}", [16, 128, 6144])
                w["ck"] = I(f"ck{i}", [128, 48])
                w["wout"] = I(f"wout{i}", [8, 128, 4096])
            else:
                w["wq"] = I(f"wq{i}", [16, 128, 4096])
                w["wk"] = I(f"wk{i}", [4, 128, 4096])
                w["wv"] = I(f"wv{i}", [2, 128, 4096])
                w["wout"] = I(f"wo{i}", [8, 128, 4096])
                w["sk"] = I(f"sk{i}", [128, 16])
            w["wgu"] = I(f"wgu{i}", [NF, 128, 4096])
            w["wd"] = I(f"wd{i}", [2 * NK, 128, 22 * 128])
            for nm in list(w.keys()):
                if nm in ("win", "wout", "wq", "wk", "wv", "wgu", "wd"):
                    w["b_" + nm] = nc.dram_tensor(f"b_{nm}{i}", list(w[nm].shape), BF16).ap()
            self.W[i] = w
        if cfg.dbg:
            self.out = nc.dram_tensor("out", [D, NT], F32, kind="ExternalOutput").ap()
            self.out_off = 0
        else:
            self.out = nc.dram_tensor("out", [D, cfg.per], F32, kind="ExternalOutput").ap()
            self.out_off = HALO
        self.XS = [nc.dram_tensor(f"xs{i}", [D, NT], F32).ap() for i in range(2)]
        self.CS = [nc.dram_tensor(f"cs{i}", [D, CTX], F32).ap() for i in range(2)]

    def ps(self):
        pool = self.ps_pool
        i = pool[self.ps_i % len(pool)]
        self.ps_i += 1
        return self.PS[i], ("PS", i)

    def fs(self):
        i = self.f_i % len(self.FS)
        self.f_i += 1
        return self.FS[i], ("FS", i)

    def bs(self):
        i = self.b_i % len(self.BSC)
        self.b_i += 1
        return self.BSC[i], ("BS", i)

    def wload(self, src, nelem, ukey=None, bsrc=None):
        i = self.w_i % NSLOT
        self.w_i += 1
        buf = self.WR[i]
        key = ("WR", i)
        if ukey is not None and ukey in self.converted:
            self.tr.dma("pool", f"wr{i}", lambda e: e.dma_start(out=buf[:, 0:nelem], in_=bsrc),
                        reads=[("WB", ukey)], writes=[key])
            return buf, key
        self.tr.dma("pool", f"wr{i}", lambda e: e.dma_start(out=buf[:, 0:nelem], in_=src), writes=[key])
        if ukey is not None:
            self.tr.dma("sp", f"ws{i}", lambda e: e.dma_start(out=bsrc, in_=buf[:, 0:nelem]),
                        reads=[key], writes=[("WB", ukey)])
            self.converted.add(ukey)
        return buf, key

    class WStream:
        def __init__(self, bld, units, depth=NSLOT - 1):
            self.b = bld
            self.units = units
            self.pos = 0
            self.loaded = []
            self.depth = depth

        def _top(self, upto):
            while len(self.loaded) < min(upto, len(self.units)):
                self.loaded.append(self.b.wload(*self.units[len(self.loaded)]))

        def prime(self):
            self._top(self.depth)

        def next(self):
            self._top(self.pos + 1)
            r = self.loaded[self.pos]
            self.pos += 1
            return r

        def after_use(self):
            self._top(self.pos + self.depth)

    def build(self):
        nc, cfg = self.nc, self.cfg
        self.declare()
        with ExitStack() as st:
            self.st = st
            tr = self.tr = Tracker(nc, st)
            sb = lambda name, shape, dt: st.enter_context(nc.sbuf_tensor(name, list(shape), dt))
            self.X = sb("X", [128, NK, 768], F32)
            self.HYraw = sb("HY", [128, 8192], F32)
            self.H = self.HYraw[:].bitcast(BF16)
            self.A = sb("A", [128, NF, 512], BF16)
            self.WR = [sb(f"WR{i}", [128, SLOT], BF16) for i in range(NSLOT)]
            self.RS = sb("RS", [128, 768], F32)
            self.FS = [sb(f"FS{i}", [128, 512], F32) for i in range(6)]
            self.BSC = [sb(f"BS{i}", [128, 512], BF16) for i in range(4)]
            self.RC = sb("RC", [128, 768], F32)
            self.RSN = sb("RSN", [128, 768], F32)
            self.KTC = sb("KTC", [128, 4, 256], BF16)
            self.VVC = sb("VVC", [128, 2, 512], BF16)
            self.CF = sb("CF", [128, 384], F32)
            self.CB = sb("CB", [128, 384], BF16)
            self.MOD = sb("MOD", [128, 96, 2], F32)
            self.VEC = sb("VEC", [128, 6, 2, NK], F32)
            self.BM = sb("BM", [128, 96], F32)
            self.NG = sb("NG", [128, 64], F32)
            self.CK = sb("CK", [128, 48], F32)
            self.SK = sb("SK", [128, 16], F32)
            self.KB = sb("KB", [128, cfg.nblk], F32)
            self.CT = sb("CT", [128, NK * 2], F32)
            self.CTB = sb("CTB", [128, NK * 2], BF16)
            self.VM = sb("VM", [128, 512], F32)
            self.PS = [st.enter_context(nc.psum_tensor(f"PS{i}", [128, 512], F32)) for i in range(8)]
            self.ps_pool = list(range(7))
            self.ps_i = self.f_i = self.b_i = self.w_i = 0
            self.converted = set()

            tr.dma("sp", "cf", lambda e: e.dma_start(out=self.CF[:], in_=self.consts), writes=["CF"])
            tr.op("dve", lambda e: e.tensor_copy(out=self.CB[:], in_=self.CF[:]), reads=["CF"], writes=["CB"])
            tr.dma("sp", "kb", lambda e: e.dma_start(out=self.KB[:], in_=self.kbias), writes=["KB"])
            tr.dma("sp", "ct", lambda e: e.dma_start(out=self.CT[:], in_=self.cT), writes=["CT"])
            tr.op("act", lambda e: e.activation(out=self.CTB[:], in_=self.CT[:], func=AF.Silu),
                  reads=["CT"], writes=["CTB"])
            self.ONES = self.CB[:, 0:128]
            self.MGE = self.CB[:, 128:256]
            self.MLE = self.CB[:, 256:384]

            src_x, src_c = (self.xT, 0, "xT"), (self.ctxT, "ctxT")
            nl = len(cfg.layers)
            for li, i in enumerate(cfg.layers):
                last = (li == nl - 1)
                if last:
                    dst_x = (self.out, self.out_off, "out")
                else:
                    dst_x = (self.XS[li % 2], 0, f"xs{li % 2}")
                dst_c = (self.CS[li % 2], f"cs{li % 2}")
                self.layer(i, src_x, dst_x, src_c, dst_c, last and i == 3)
                src_x, src_c = dst_x, dst_c

            for name, sem, val in [self.last_store]:
                nc.sync.wait_ge(sem, val)
        return nc

    def layer(self, i, src_x, dst_x, src_c, dst_c, final):
        cfg, tr = self.cfg, self.tr
        w = self.W[i]
        conv = (i % 2 == 0)
        tr.dma("sp", "bm", lambda e: e.dma_start(out=self.BM[:], in_=w["bm"]), writes=["BM"])
        tr.dma("sp", "ng", lambda e: e.dma_start(out=self.NG[:], in_=w["ng"]), writes=["NG"])
        if conv:
            tr.dma("sp", "ck", lambda e: e.dma_start(out=self.CK[:], in_=w["ck"]), writes=["CK"])
        else:
            tr.dma("sp", "sk", lambda e: e.dma_start(out=self.SK[:], in_=w["sk"]), writes=["SK"])
            tr.op("act", lambda e: e.activation(out=self.SK[:], in_=self.SK[:], func=AF.Exp),
                  reads=["SK"], writes=["SK"])
        self.modulation(i)

        lo, hi = cfg.lrange(i)
        tiles = []
        step = 510 if conv else 512
        t = lo
        while t < hi:
            n = min(step, hi - t)
            tiles.append((False, t, n))
            t += n
        if cfg.tiles is not None:
            tiles = tiles[:cfg.tiles[i]]
        tiles = [(True, 0, CTX)] + tiles
        units = []
        for (is_ctx, _, _) in tiles:
            units += self.tile_units(i, is_ctx, final)
        ws = self.WStream(self, units)
        ws.prime()
        for (is_ctx, t0, n) in tiles:
            self.tile(i, ws, is_ctx, t0, n, src_x, dst_x, src_c, dst_c, final)

    def tile_units(self, i, is_ctx, final):
        w = self.W[i]
        u = []
        U = lambda nm, idx, n: (w[nm][idx], n, (i, nm, idx), w["b_" + nm][idx])
        if i % 2 == 0:
            for j in range(16):
                u += [(w["win"][j][:, 0:4096], 4096, (i, "winA", j), w["b_win"][j][:, 0:4096]),
                      (w["win"][j][:, 4096:6144], 2048, (i, "winB", j), w["b_win"][j][:, 4096:6144])]
            u += [U("wout", l, 4096) for l in range(8)]
        else:
            u += [U("wk", g, 4096) for g in range(4)]
            u += [U("wv", l, 4096) for l in range(2)]
            if not (is_ctx and final):
                u += [U("wq", j, 4096) for j in range(16)]
                u += [U("wout", l, 4096) for l in range(8)]
        if not (is_ctx and final):
            u += [U("wgu", f, 4096) for f in range(NF)]
            u += [U("wd", d, 22 * 128) for d in range(2 * NK)]
        return u

    def modulation(self, i):
        tr = self.tr
        w = self.W[i]
        units = [(w["mod"][l], SLOT) for l in range(48)]
        ws = self.WStream(self, units)
        ws.prime()
        pst, psk = self.ps()
        ctb = self.CTB[:].rearrange("p (k r) -> p k r", r=2)
        for l in range(48):
            buf, key = ws.next()
            wv = buf[:, 0:SLOT].rearrange("p (a k c) -> p a k c", a=2, k=NK)

            def mm(e):
                ins = None
                for a in range(2):
                    ch = 2 * l + a
                    for k in range(NK):
                        ins = e.matmul(pst[:, 2 * ch:2 * ch + 2], wv[:, a, k, :], ctb[:, k, :],
                                       start=(k == 0), stop=(k == NK - 1))
                return ins
            tr.op("pe", mm, reads=[key, "CTB"], writes=[psk])
            ws.after_use()
        modp = pst[:, 0:192].rearrange("p (c r) -> p c r", r=2)
        for r in range(2):
            tr.op("dve", lambda e: e.tensor_tensor(out=self.MOD[:, :, r], in0=modp[:, :, r], in1=self.BM[:], op=ALU.add),
                  reads=[psk, "BM"], writes=["MOD"])
        M = self.MOD
        V = self.VEC
        ng = self.NG[:].rearrange("p (n k) -> p n k", n=4)
        for r in range(2):
            sec = lambda s: M[:, 16 * s:16 * s + 16, r]
            for (dst, scs, gn) in ((0, 1, 0), (3, 4, 2)):
                tr.op("dve", lambda e: e.scalar_tensor_tensor(out=V[:, dst, r, :], in0=sec(scs), scalar=1.0, in1=ng[:, gn, :],
                                                              op0=ALU.add, op1=ALU.mult), reads=["MOD", "NG"], writes=["VEC"])
            for (dst, s) in ((1, 0), (4, 3)):
                tr.op("dve", lambda e: e.tensor_copy(out=V[:, dst, r, :], in_=sec(s)), reads=["MOD"], writes=["VEC"])
            for (dst, s, gn) in ((2, 2, 1), (5, 5, 3)):
                tr.op("dve", lambda e: e.tensor_tensor(out=V[:, dst, r, :], in0=sec(s), in1=ng[:, gn, :], op=ALU.mult),
                      reads=["MOD", "NG"], writes=["VEC"])

    def ss_start(self, n):
        self.ss_n, self.ss_k, self.ss_pend = n, 0, None

    def _ss_flush(self):
        if self.ss_pend is None:
            return
        sq, sqk = self.ss_pend
        k, n = self.ss_k, self.ss_n
        self.tr.op("pe", lambda e: e.matmul(self.PS[7][:, 0:n], self.ONES, sq[:, 0:n], start=(k == 0), stop=(k == NK - 1),
                                            skip_group_check=True), reads=[sqk, "CB"], writes=[("PS", 7)])
        self.ss_k += 1
        self.ss_pend = None

    def ss_push(self, src, key):
        n = self.ss_n
        sq, sqk = self.bs()
        self.tr.op("act", lambda e: e.activation(out=sq[:, 0:n], in_=src, func=AF.Square), reads=[key], writes=[sqk])
        self._ss_flush()
        self.ss_pend = (sq, sqk)

    def ss_finish(self):
        self._ss_flush()
        assert self.ss_k == NK
        tr, n = self.tr, self.ss_n
        rs = self.RS[:, 0:n]
        tr.op("dve", lambda e: e.tensor_scalar(out=rs, in0=self.PS[7][:, 0:n], scalar1=1.0 / D, scalar2=EPS,
                                               op0=ALU.mult, op1=ALU.add), reads=[("PS", 7)], writes=["RS"])
        tr.op("act", lambda e: e.activation(out=rs, in_=rs, func=AF.Sqrt), reads=["RS"], writes=["RS"])
        tr.op("dve", lambda e: e.reciprocal(out=rs, in_=rs), reads=["RS"], writes=["RS"])

    def rstd(self, srcs, keys, n, rs_off):
        tr = self.tr
        c = 0
        while c < n:
            m = min(512, n - c)
            pst, psk = self.ps()
            for k in range(NK):
                sq, sqk = self.bs()
                tr.op("act", lambda e: e.activation(out=sq[:, 0:m], in_=srcs[k][:, c:c + m], func=AF.Square),
                      reads=[keys[k]], writes=[sqk])
                tr.op("pe", lambda e: e.matmul(pst[:, 0:m], self.ONES, sq[:, 0:m], start=(k == 0), stop=(k == NK - 1),
                                               skip_group_check=True),
                      reads=[sqk, "CB"], writes=[psk])
            rs = self.RS[:, rs_off + c:rs_off + c + m]
            tr.op("dve", lambda e: e.tensor_scalar(out=rs, in0=pst[:, 0:m], scalar1=1.0 / D, scalar2=EPS,
                                                   op0=ALU.mult, op1=ALU.add), reads=[psk], writes=["RS"])
            tr.op("act", lambda e: e.activation(out=rs, in_=rs, func=AF.Sqrt), reads=["RS"], writes=["RS"])
            tr.op("dve", lambda e: e.reciprocal(out=rs, in_=rs), reads=["RS"], writes=["RS"])
            c += m

    def prenorm(self, n, xc, which, r):
        tr = self.tr
        srcs = [self.X[:, k, xc:xc + n] for k in range(NK)]
        keys = [("X", k) for k in range(NK)]
        if which == 1:
            self.rstd(srcs, keys, n, 0)
        else:
            self.ss_finish()
        V = self.VEC
        gs, sh = (0, 1) if which == 1 else (3, 4)
        Hv = self.H.rearrange("p (k t) -> p k t", k=NK)
        for k in range(NK):
            c = 0
            while c < n:
                m = min(512, n - c)
                tmp, tk = self.fs()
                tr.op("dve", lambda e: e.tensor_tensor(out=tmp[:, 0:m], in0=srcs[k][:, c:c + m], in1=self.RS[:, c:c + m], op=ALU.mult),
                      reads=[("X", k), "RS"], writes=[tk])
                tr.op("act", lambda e: e.activation(out=Hv[:, k, c:c + m], in_=tmp[:, 0:m], func=AF.Identity,
                                                    scale=V[:, gs, r, k:k + 1], bias=V[:, sh, r, k:k + 1]),
                      reads=[tk, "VEC"], writes=[("HY", k)])
                c += m
        return Hv

    def postnorm(self, n, xc, which, r):
        tr = self.tr
        Yv = self.HYraw[:].rearrange("p (k t) -> p k t", k=NK)
        srcs = [Yv[:, d, 0:n] for d in range(NK)]
        keys = [("HY", d) for d in range(NK)]
        self.ss_finish()
        if which == 1:
            self.ss_start(n)
        V = self.VEC
        gg = 2 if which == 1 else 5
        for d in range(NK):
            tmp, tk = self.fs()
            tr.op("dve", lambda e: e.scalar_tensor_tensor(out=tmp[:, 0:n], in0=srcs[d], scalar=V[:, gg, r, d:d + 1], in1=self.RS[:, 0:n],
                                                          op0=ALU.mult, op1=ALU.mult), reads=[("HY", d), "RS", "VEC"], writes=[tk])
            xs = self.X[:, d, xc:xc + n]
            tr.op("pool", lambda e: e.tensor_tensor(out=xs, in0=xs, in1=tmp[:, 0:n], op=ALU.add),
                  reads=[tk, ("X", d)], writes=[("X", d)])
            if which == 1:
                self.ss_push(xs, ("X", d))

    def h_to_y(self):
        pass

    def y_to_h(self):
        pass

    def out_proj(self, ws, acts, akeys, n):
        tr = self.tr
        Yv = self.HYraw[:].rearrange("p (k t) -> p k t", k=NK)
        self.ss_start(n)
        for l in range(8):
            buf, key = ws.next()
            wv = buf[:, 0:4096].rearrange("p (a j c) -> p a j c", a=2, j=NK)
            for a in range(2):
                d = 2 * l + a
                pst, psk = self.ps()

                def mm(e):
                    ins = None
                    for j in range(NK):
                        ins = e.matmul(pst[:, 0:n], wv[:, a, j, :], acts[j], start=(j == 0), stop=(j == NK - 1))
                    return ins
                tr.op("pe", mm, reads=[key] + akeys, writes=[psk])
                tr.op("act", lambda e: e.copy(out=Yv[:, d, 0:n], in_=pst[:, 0:n]), reads=[psk], writes=[("HY", d)])
                self.ss_push(Yv[:, d, 0:n], ("HY", d))
            ws.after_use()

    def tile(self, i, ws, is_ctx, t0, n, src_x, dst_x, src_c, dst_c, final):
        cfg, tr = self.cfg, self.tr
        conv = (i % 2 == 0)
        r = 1 if is_ctx else 0
        if is_ctx:
            ncol, xc = CTX, 0
            src = src_c[0].rearrange("(k p) t -> p k t", p=128)
            skey = ("DR", src_c[1])
        elif conv:
            ncol, xc = n + 2, 1
            sa, so, sn = src_x
            skey = ("DR", sn)
            src = sa.rearrange("(k p) t -> p k t", p=128)[:, :, t0 - 1 - so:t0 - 1 - so + ncol]
        else:
            ncol, xc = n + 256, 128
            sa, so, sn = src_x
            skey = ("DR", sn)
            src = sa.rearrange("(k p) t -> p k t", p=128)[:, :, t0 - 128 - so:t0 - 128 - so + ncol]
        xkeys = [("X", k) for k in range(NK)]
        tr.dma("sp", "xin", lambda e: e.dma_start(out=self.X[:, :, 0:ncol], in_=src), reads=[skey], writes=xkeys)

        Hv = self.prenorm(ncol, 0, 1, r)
        if conv:
            self.conv_mixer(i, ws, is_ctx, t0, n, ncol, Hv)
        else:
            self.attn_mixer(i, ws, is_ctx, t0, n, ncol, Hv, final)
        if is_ctx and final:
            return
        self.postnorm(n, xc, 1, r)
        self.y_to_h()

        Hv = self.prenorm(n, xc, 2, r)
        self.ffn(ws, n, Hv)
        self.postnorm(n, xc, 2, r)
        self.y_to_h()

        if is_ctx:
            dst = dst_c[0].rearrange("(k p) t -> p k t", p=128)
            dkey = ("DR", dst_c[1])
        else:
            da, do, dn = dst_x
            dst = da.rearrange("(k p) t -> p k t", p=128)[:, :, t0 - do:t0 - do + n]
            dkey = ("DR", dn)
        self.last_store = tr.dma("sp", "xout", lambda e: e.dma_start(out=dst, in_=self.X[:, :, xc:xc + n]),
                                 reads=xkeys, writes=[dkey])

    def ffn(self, ws, n, Hv):
        tr = self.tr
        hkeys = [("HY", k) for k in range(NK)]
        for f in range(NF):
            buf, key = ws.next()
            wv = buf[:, 0:4096].rearrange("p (a k c) -> p a k c", a=2, k=NK)
            pg, pgk = self.ps()
            pu, puk = self.ps()

            def mm(e, dst, a):
                ins = None
                for k in range(NK):
                    ins = e.matmul(dst[:, 0:n], wv[:, a, k, :], Hv[:, k, 0:n], start=(k == 0), stop=(k == NK - 1))
                return ins
            tr.op("pe", lambda e: mm(e, pg, 0), reads=[key] + hkeys, writes=[pgk])
            tr.op("pe", lambda e: mm(e, pu, 1), reads=[key] + hkeys, writes=[puk])
            sg, sgk = self.fs()
            tr.op("act", lambda e: e.activation(out=sg[:, 0:n], in_=pg[:, 0:n], func=AF.Silu), reads=[pgk], writes=[sgk])
            tr.op("dve", lambda e: e.tensor_tensor(out=self.A[:, f, 0:n], in0=pu[:, 0:n], in1=sg[:, 0:n], op=ALU.mult),
                  reads=[puk, sgk], writes=[("A", f)])
            ws.after_use()
        self.h_to_y()
        Yv = self.HYraw[:].rearrange("p (k t) -> p k t", k=NK)
        akeys = [("A", f) for f in range(NF)]
        self.ss_start(n)
        for d in range(NK):
            pst, psk = self.ps()
            for hf in range(2):
                buf, key = ws.next()
                wv = buf[:, 0:22 * 128].rearrange("p (f c) -> p f c", f=22)

                def mmd(e):
                    ins = None
                    for f in range(22):
                        ff = 22 * hf + f
                        ins = e.matmul(pst[:, 0:n], wv[:, f, :], self.A[:, ff, 0:n], start=(ff == 0), stop=(ff == NF - 1),
                                       skip_group_check=True)
                    return ins
                tr.op("pe", mmd, reads=[key] + akeys, writes=[psk])
                ws.after_use()
            tr.op("act", lambda e: e.copy(out=Yv[:, d, 0:n], in_=pst[:, 0:n]), reads=[psk], writes=[("HY", d)])
            self.ss_push(Yv[:, d, 0:n], ("HY", d))

    def conv_mixer(self, i, ws, is_ctx, t0, n, ncol, Hv):
        tr = self.tr
        hkeys = [("HY", k) for k in range(NK)]
        ck = self.CK[:].rearrange("p (j t) -> p j t", t=3)
        boundary = False
        if not is_ctx:
            lo, hi = self.cfg.lrange(i)
            boundary = (t0 - 1 < HALO) or (t0 + n + 1 > self.cfg.NTOK - HALO)
            if boundary:
                tr.dma("sp", "vm", lambda e: e.dma_start(out=self.VM[:, 0:ncol], in_=self.vmask[:, t0 - 1:t0 - 1 + ncol]), writes=["VM"])
        for j in range(NK):
            buf, key = ws.next()
            wvA = buf[:, 0:4096].rearrange("p (a k c) -> p a k c", a=2, k=NK)
            buf2, key2 = ws.next()
            wvB = buf2[:, 0:2048].rearrange("p (a k c) -> p a k c", a=1, k=NK)
            pss = [self.ps() for _ in range(3)]

            def mm(e, dst, wv, a):
                ins = None
                for k in range(NK):
                    ins = e.matmul(dst[:, 0:ncol], wv[:, a, k, :], Hv[:, k, 0:ncol], start=(k == 0), stop=(k == NK - 1))
                return ins
            tr.op("pe", lambda e: mm(e, pss[0][0], wvA, 0), reads=[key] + hkeys, writes=[pss[0][1]])
            tr.op("pe", lambda e: mm(e, pss[1][0], wvA, 1), reads=[key] + hkeys, writes=[pss[1][1]])
            tr.op("pe", lambda e: mm(e, pss[2][0], wvB, 0), reads=[key2] + hkeys, writes=[pss[2][1]])
            (pb, pbk), (pc, pck), (pv, pvk) = pss
            cs, csk = self.fs()
            tr.op("act", lambda e: e.copy(out=cs[:, 0:ncol], in_=pc[:, 0:ncol]), reads=[pck], writes=[csk])
            u, uk = self.fs()
            uo = 1 if is_ctx else 0
            if is_ctx:
                tr.op("pool", lambda e: e.memset(u[:, 0:ncol + 2], 0.0), writes=[uk])
            tr.op("dve", lambda e: e.tensor_tensor(out=u[:, uo:uo + ncol], in0=pv[:, 0:ncol], in1=cs[:, 0:ncol], op=ALU.mult),
                  reads=[pvk, csk], writes=[uk])
            if boundary:
                tr.op("pool", lambda e: e.tensor_tensor(out=u[:, 0:ncol], in0=u[:, 0:ncol], in1=self.VM[:, 0:ncol], op=ALU.mult),
                      reads=[uk, "VM"], writes=[uk])
            cv, cvk = self.fs()
            tr.op("act", lambda e: e.activation(out=cv[:, 0:n], in_=u[:, 0:n], func=AF.Identity, scale=ck[:, j, 0:1]),
                  reads=[uk, "CK"], writes=[cvk])
            for tap in (1, 2):
                tr.op("dve", lambda e: e.scalar_tensor_tensor(out=cv[:, 0:n], in0=u[:, tap:tap + n], scalar=ck[:, j, tap:tap + 1],
                                                               in1=cv[:, 0:n], op0=ALU.mult, op1=ALU.add),
                      reads=[uk, cvk, "CK"], writes=[cvk])
            bo = 0 if is_ctx else 1
            tr.op("dve", lambda e: e.tensor_tensor(out=self.A[:, j, 0:n], in0=pb[:, bo:bo + n], in1=cv[:, 0:n], op=ALU.mult),
                  reads=[pbk, cvk], writes=[("A", j)])
            ws.after_use()
        self.h_to_y()
        self.out_proj(ws, [self.A[:, j, 0:n] for j in range(NK)], [("A", j) for j in range(NK)], n)

    def attn_mixer(self, i, ws, is_ctx, t0, n, ncol, Hv, final):
        tr, cfg = self.tr, self.cfg
        hkeys = [("HY", k) for k in range(NK)]
        A = self.A
        nkb = ncol // 128
        nqb = n // 128
        qoff = 0 if is_ctx else 128
        KT = A[:, 32:38, :].rearrange("p s t -> p (s t)").rearrange("p (g t) -> p g t", g=4)
        VV = A[:, 38:44, :]
        akq = [("A", s) for s in range(0, 16)]
        ako = [("A", s) for s in range(16, 32)]
        akk = [("A", s) for s in range(32, 38)]
        akv = [("A", s) for s in range(38, 44)]
        if is_ctx:
            KTt, VVt, kkeys, vkeys = self.KTC, self.VVC, ["KTC"], ["VVC"]
        else:
            KTt, VVt, kkeys, vkeys = KT, VV, akk, akv
            tr.dma("sp", "rp", lambda e: [e.dma_start(out=self.RC[:, 0:ncol], in_=self.ropeC[:, t0 - 128:t0 - 128 + ncol]),
                                           e.dma_start(out=self.RSN[:, 0:ncol], in_=self.ropeS[:, t0 - 128:t0 - 128 + ncol])],
                   writes=["ROPE"])

        def proj2(buf, key, cols):
            wv = buf[:, 0:4096].rearrange("p (a k c) -> p a k c", a=2, k=NK)
            outs = []
            for a in range(2):
                pst, psk = self.ps()

                def mm(e):
                    ins = None
                    for k in range(NK):
                        ins = e.matmul(pst[:, 0:cols[1] - cols[0]], wv[:, a, k, :], Hv[:, k, cols[0]:cols[1]],
                                       start=(k == 0), stop=(k == NK - 1))
                    return ins
                tr.op("pe", mm, reads=[key] + hkeys, writes=[psk])
                outs.append((pst, psk))
                if is_ctx:
                    break
            return outs

        def rope(outs, cols, dst, dkey):
            m = cols[1] - cols[0]
            if is_ctx:
                (p0, k0), = outs
                tr.op("act", lambda e: e.copy(out=dst, in_=p0[:, 0:m]), reads=[k0], writes=dkey)
                return
            (p0, k0), (p1, k1) = outs
            t1, t1k = self.fs()
            t2, t2k = self.fs()
            tr.op("dve", lambda e: e.tensor_tensor(out=t1[:, 0:m], in0=p0[:, 0:m], in1=self.RC[:, cols[0]:cols[1]], op=ALU.mult),
                  reads=[k0, "ROPE"], writes=[t1k])
            tr.op("dve", lambda e: e.tensor_tensor(out=t2[:, 0:m], in0=p1[:, 0:m], in1=self.RSN[:, cols[0]:cols[1]], op=ALU.mult),
                  reads=[k1, "ROPE"], writes=[t2k])
            tr.op("pool", lambda e: e.tensor_tensor(out=dst, in0=t1[:, 0:m], in1=t2[:, 0:m], op=ALU.add),
                  reads=[t1k, t2k], writes=dkey)

        for g in range(4):
            buf, key = ws.next()
            c = 0
            while c < ncol:
                m = min(512, ncol - c)
                outs = proj2(buf, key, (c, c + m))
                rope(outs, (c, c + m), KTt[:, g, c:c + m], ["KTC"] if is_ctx else
                     [("A", 32 + q) for q in range((g * 768 + c) // 512, (g * 768 + c + m - 1) // 512 + 1)])
                c += m
            ws.after_use()
        for l in range(2):
            buf, key = ws.next()
            wv = buf[:, 0:4096].rearrange("p (k c) -> p k c", k=NK)
            for kb in range(nkb):
                pst, psk = self.ps()

                def mm(e):
                    ins = None
                    for k in range(NK):
                        ins = e.matmul(pst[:, 0:256], Hv[:, k, kb * 128:(kb + 1) * 128], wv[:, k, :],
                                       start=(k == 0), stop=(k == NK - 1))
                    return ins
                tr.op("pe", mm, reads=[key] + hkeys, writes=[psk])
                tr.op("act", lambda e: e.copy(out=VVt[:, kb, l * 256:(l + 1) * 256], in_=pst[:, 0:256]),
                      reads=[psk], writes=["VVC"] if is_ctx else [("A", 38 + kb)])
            ws.after_use()
        if is_ctx and final:
            return
        for j in range(NK):
            buf, key = ws.next()
            outs = proj2(buf, key, (qoff, qoff + n))
            rope(outs, (qoff, qoff + n), A[:, j, 0:n], [("A", j)])
            ws.after_use()

        self.ps_pool = [0, 1, 2, 3]
        items = []
        for h in range(32):
            j, half = h // 2, h % 2
            g = h // 8
            rows = slice(64 * half, 64 * half + 64)
            chunks = []
            for c in range(2):
                chunks.append((self.KTC[rows, g, c * 128:(c + 1) * 128], self.VVC[:, c, g * 128:(g + 1) * 128],
                               (0, n), None, [], ["KTC", "VVC"]))
            if not is_ctx:
                blk0 = (t0 - 1) // 128 - 1
                for kb in range(nkb):
                    q0, q1 = max(0, kb - 2), min(nqb - 1, kb)
                    masks = []
                    if kb - 2 >= 0 and kb - 2 <= nqb - 1:
                        masks.append(((kb - 2 - q0) * 128, self.MLE))
                    if kb <= nqb - 1:
                        masks.append(((kb - q0) * 128, self.MGE))
                    chunks.append((KT[rows, g, kb * 128:(kb + 1) * 128], VV[:, kb, g * 128:(g + 1) * 128],
                                   (q0 * 128, (q1 + 1) * 128), self.KB[:, blk0 + kb:blk0 + kb + 1], masks, akk + akv))
            for ci, ch in enumerate(chunks):
                items.append((h, j, rows, ci, len(chunks), ch))

        def front(it):
            h, j, rows, ci, nch, (kap, vap, (c0, c1), bias, masks, rk) = it
            m = c1 - c0
            pst, psk = self.ps()
            tr.op("pe", lambda e: e.matmul(pst[:, 0:m], kap, A[rows, j, c0:c1], start=True, stop=True),
                  reads=rk + [("A", j)], writes=[psk])
            pt, ptk = self.bs()
            if bias is None:
                tr.op("act", lambda e: e.activation(out=pt[:, 0:m], in_=pst[:, 0:m], func=AF.Exp, scale=0.125),
                      reads=[psk], writes=[ptk])
            else:
                tr.op("act", lambda e: e.activation(out=pt[:, 0:m], in_=pst[:, 0:m], func=AF.Exp, scale=0.125, bias=bias),
                      reads=[psk, "KB"], writes=[ptk])
            for (mo, mk) in masks:
                tr.op("pool", lambda e: e.tensor_tensor(out=pt[:, mo:mo + 128], in0=pt[:, mo:mo + 128], in1=mk, op=ALU.mult),
                      reads=[ptk, "CB"], writes=[ptk])
            return (it, pt, ptk)

        def back(ent):
            (h, j, rows, ci, nch, (kap, vap, (c0, c1), bias, masks, rk)), pt, ptk = ent
            m = c1 - c0
            po, pok = self.PS[4 + (h % 2)], ("PS", 4 + (h % 2))
            pd, pdk = self.PS[6 + (h % 2)], ("PS", 6 + (h % 2))
            first = (ci == 0)
            lastc = (ci == nch - 1)
            tr.op("pe", lambda e: e.matmul(po[:, c0:c1], vap, pt[:, 0:m], start=first, stop=lastc, skip_group_check=True),
                  reads=[ptk] + rk, writes=[pok])
            tr.op("pe", lambda e: e.matmul(pd[:, c0:c1], self.ONES, pt[:, 0:m], start=first, stop=lastc, skip_group_check=True),
                  reads=[ptk, "CB"], writes=[pdk])
            if lastc:
                rd, rdk = self.fs()
                tr.op("dve", lambda e: e.tensor_scalar(out=rd[rows, 0:n], in0=pd[rows, 0:n], scalar1=self.SK[rows, j:j + 1], scalar2=None, op0=ALU.add),
                      reads=[pdk, "SK"], writes=[rdk])
                tr.op("dve", lambda e: e.reciprocal(out=rd[rows, 0:n], in_=rd[rows, 0:n]), reads=[rdk], writes=[rdk])
                tr.op("dve", lambda e: e.tensor_tensor(out=A[rows, 16 + j, 0:n], in0=po[rows, 0:n], in1=rd[rows, 0:n], op=ALU.mult),
                      reads=[pok, rdk], writes=[("A", 16 + j)])

        LA = 2
        pend = []
        for idx in range(len(items) + LA):
            if idx < len(items):
                pend.append(front(items[idx]))
            if idx >= LA:
                back(pend.pop(0))
        self.ps_pool = list(range(7))
        self.h_to_y()
        self.out_proj(ws, [A[:, 16 + j, 0:n] for j in range(NK)], ako, n)


def _partner():
    d = np.arange(64)
    within = d % 32
    return np.where(within < 16, d + 16, d - 16)


def _unit_kc(wcols):
    return wcols.reshape(NK, 128, -1).transpose(1, 0, 2)


def prep_weights(inp, layers):
    out = {}
    f = lambda a: np.ascontiguousarray(a, dtype=np.float32)
    part = _partner()
    for i in layers:
        wm = np.asarray(inp["w_mod"][i])
        out[f"wmod{i}"] = f(wm.reshape(NK, 128, 48, 2, 128).transpose(2, 1, 3, 0, 4)).reshape(48, 128, SLOT)
        out[f"bm{i}"] = f(np.asarray(inp["b_mod"][i]).reshape(96, 128).T)
        out[f"ng{i}"] = f(np.asarray(inp["norm_g"][i]).reshape(4, NK, 128).transpose(2, 0, 1)).reshape(128, 64)
        j = i // 2
        if i % 2 == 0:
            wi = np.asarray(inp["conv_w_in"][j])
            out[f"win{i}"] = f(wi.reshape(NK, 128, 3, NK, 128).transpose(3, 1, 2, 0, 4)).reshape(16, 128, 6144)
            out[f"ck{i}"] = f(np.asarray(inp["conv_k"][j]).reshape(3, NK, 128).transpose(2, 1, 0)).reshape(128, 48)
            wo = np.asarray(inp["conv_w_out"][j])
            out[f"wout{i}"] = f(wo.reshape(NK, 128, 8, 2, 128).transpose(2, 1, 3, 0, 4)).reshape(8, 128, 4096)
        else:
            wqkv = np.asarray(inp["attn_w_qkv"][j])
            wq = wqkv[:, :2048]
            colperm = (np.arange(2048) // 64) * 64 + part[np.arange(2048) % 64]
            wq2 = np.stack([wq, wq[:, colperm]], axis=0)
            out[f"wq{i}"] = f(wq2.reshape(2, NK, 128, NK, 128).transpose(3, 2, 0, 1, 4)).reshape(16, 128, 4096)
            wk = wqkv[:, 2048:2304].reshape(2048, 4, 64)
            wkd = np.concatenate([wk, wk], axis=2)
            wkp = wk[:, :, part]
            wkpd = np.concatenate([wkp, wkp], axis=2)
            wk2 = np.stack([wkd, wkpd], axis=0)
            out[f"wk{i}"] = f(wk2.reshape(2, NK, 128, 4, 128).transpose(3, 2, 0, 1, 4)).reshape(4, 128, 4096)
            wv = wqkv[:, 2304:2560].reshape(2048, 4, 64)
            wvd = np.concatenate([wv, wv], axis=2).reshape(2048, 2, 256)
            out[f"wv{i}"] = f(wvd.reshape(NK, 128, 2, 256).transpose(2, 1, 0, 3)).reshape(2, 128, 4096)
            wo = np.asarray(inp["attn_w_o"][j])
            out[f"wo{i}"] = f(wo.reshape(NK, 128, 8, 2, 128).transpose(2, 1, 3, 0, 4)).reshape(8, 128, 4096)
            sk = np.asarray(inp["attn_sink"][j])
            out[f"sk{i}"] = f(sk[2 * np.arange(16)[None, :] + (np.arange(128)[:, None] // 64)])
        wg = np.asarray(inp["ffn_w_gate"][i]).reshape(NK, 128, NF, 128).transpose(2, 1, 0, 3)
        wu = np.asarray(inp["ffn_w_up"][i]).reshape(NK, 128, NF, 128).transpose(2, 1, 0, 3)
        out[f"wgu{i}"] = f(np.stack([wg, wu], axis=2)).reshape(NF, 128, 4096)
        out[f"wd{i}"] = f(np.asarray(inp["ffn_w_down"][i]).reshape(2, 22, 128, NK, 128).transpose(3, 0, 2, 1, 4)).reshape(2 * NK, 128, 22 * 128)
    return out


def core_inputs(inp, cfg, c):
    cpb = cfg.NC // B
    b, s = c // cpb, c % cpb
    lo = s * cfg.per - HALO
    NT = cfg.NTOK
    gt = lo + np.arange(NT)
    valid = (gt >= 0) & (gt < S)
    x = np.asarray(inp["x"][b])
    xT = np.zeros((D, NT), np.float32)
    xT[:, valid] = x[gt[valid]].T
    m = {"xT": xT, "ctxT": np.ascontiguousarray(np.asarray(inp["ctx"][b]).T)}
    cc = np.stack([np.asarray(inp["c"][b]), np.asarray(inp["c_ctx"])], axis=-1)
    m["cT"] = np.ascontiguousarray(cc.reshape(NK, 128, 2).transpose(1, 0, 2)).reshape(128, NK * 2).astype(np.float32)
    p = np.arange(128)
    d = p % 64
    fidx = d % 16
    inv = (np.float32(10000.0) ** (-np.arange(16, dtype=np.float32) / np.float32(16))).astype(np.float32)
    gtc = np.clip(gt, 0, S - 1)
    pos = np.where((d // 32)[:, None] == 0, (gtc // 64)[None, :], (gtc % 64)[None, :]).astype(np.float32)
    ang = (pos * inv[fidx][:, None]).astype(np.float32)
    sign = np.where((d % 32) < 16, -1.0, 1.0).astype(np.float32)[:, None]
    m["ropeC"] = np.cos(ang).astype(np.float32)
    m["ropeS"] = (np.sin(ang).astype(np.float32) * sign).astype(np.float32)
    kb = np.zeros((128, cfg.nblk), np.float32)
    tl = 1 + 128 * np.arange(cfg.nblk)[None, :] + np.arange(128)[:, None]
    kb[~valid[tl]] = NEGB
    m["kbias"] = kb
    m["vmask"] = np.broadcast_to(valid.astype(np.float32)[None, :], (128, NT)).copy()
    kk = np.arange(128)[:, None]
    qq = np.arange(128)[None, :]
    m["consts"] = np.concatenate([np.ones((128, 128)), (kk >= qq), (kk <= qq)], axis=1).astype(np.float32)
    return m


_CFG = Cfg(nc_cores=8)


def kernel(**inputs):
    cfg = _CFG
    bld = Builder(cfg)
    nc = bld.build()
    wts = prep_weights(inputs, cfg.layers)
    in_maps = []
    for c in range(cfg.NC):
        m = core_inputs(inputs, cfg, c)
        m.update(wts)
        in_maps.append(m)
    res = run_bass_kernel_spmd(nc, in_maps, core_ids=list(range(cfg.NC)))
    cpb = cfg.NC // B
    out = np.empty((B, S, D), np.float32)
    for c in range(cfg.NC):
        b, s = c // cpb, c % cpb
        out[b, s * cfg.per:(s + 1) * cfg.per, :] = res.results[c]["out"].T
    return out
```

```python
import numpy as np
from contextlib import ExitStack
import concourse.bass as bass
import concourse.mybir as mybir
from concourse.bass_utils import run_bass_kernel_spmd

F32 = mybir.dt.float32
BF16 = mybir.dt.bfloat16
ALU = mybir.AluOpType
AF = mybir.ActivationFunctionType

D = 2048
NK = 16
S = 8192
B = 2
CTX = 256
DFF = 5632
NF = 44
HALO = 385
EPS = 1e-6
NEGB = -30000.0
SLOT = 4096
NSLOT = 5


class Cfg:
    def __init__(self, nc_cores=4, layers=(0, 1, 2, 3), tiles=None, dbg=False):
        self.NC = nc_cores
        self.layers = tuple(layers)
        self.per = B * S // nc_cores
        self.NTOK = self.per + 2 * HALO
        self.nblk = (self.NTOK - 2) // 128
        self.tiles = tiles
        self.dbg = dbg

    def lrange(self, i):
        lo = 1 + 128 * i
        hi = self.NTOK - 1 - 128 * i
        return lo, hi


class Tracker:
    def __init__(self, nc, stack):
        self.nc = nc
        self.engs = {"pe": nc.tensor, "act": nc.scalar, "dve": nc.vector,
                     "pool": nc.gpsimd, "sp": nc.sync}
        self.sem = {}
        self.cnt = {}
        self.stack = stack
        for e in self.engs:
            self.sem[e] = stack.enter_context(nc.semaphore("s_" + e))
            self.cnt[e] = 0
        self.dsem = {}
        self.dcnt = {}
        self.waited = {e: {} for e in self.engs}
        self.lastw = {}
        self.readers = {}

    def _deps(self, reads, writes):
        deps = {}

        def add(tok):
            if tok is None:
                return
            name, sem, val = tok
            if name not in deps or deps[name][1] < val:
                deps[name] = (sem, val)
        for k in reads:
            add(self.lastw.get(k))
        for k in writes:
            add(self.lastw.get(k))
            for r in self.readers.get(k, ()):
                add(r)
        return deps

    def _emit_waits(self, eng, deps):
        e = self.engs[eng]
        w = self.waited[eng]
        for name, (sem, val) in deps.items():
            if eng == "pe" and name == "pe":
                continue
            if w.get(name, 0) >= val:
                continue
            e.wait_ge(sem, val)
            w[name] = val

    def _commit(self, tok, reads, writes):
        for k in writes:
            self.lastw[k] = tok
            self.readers[k] = []
        for k in reads:
            self.readers.setdefault(k, []).append(tok)

    def op(self, eng, fn, reads=(), writes=()):
        self._emit_waits(eng, self._deps(reads, writes))
        ins = fn(self.engs[eng])
        self.cnt[eng] += 1
        ins.then_inc(self.sem[eng], 1)
        tok = (eng, self.sem[eng], self.cnt[eng])
        self._commit(tok, reads, writes)
        return tok

    def dma(self, queue, slot, fn, reads=(), writes=()):
        if slot not in self.dsem:
            self.dsem[slot] = self.stack.enter_context(self.nc.semaphore("d_" + slot))
            self.dcnt[slot] = 0
        self._emit_waits(queue, self._deps(reads, writes))
        inss = fn(self.engs[queue])
        if not isinstance(inss, (list, tuple)):
            inss = [inss]
        for ins in inss:
            ins.then_inc(self.dsem[slot], 16)
            self.dcnt[slot] += 16
        tok = ("d_" + slot, self.dsem[slot], self.dcnt[slot])
        self._commit(tok, reads, writes)
        return tok

    def retire(self, keys):
        toks = []
        for k in keys:
            if self.lastw.get(k) is not None:
                toks.append(self.lastw[k])
            toks.extend(self.readers.get(k, ()))
        return toks

    def inherit(self, keys, toks):
        for k in keys:
            self.readers.setdefault(k, []).extend(toks)


class Builder:
    def __init__(self, cfg):
        self.cfg = cfg
        self.nc = bass.Bass("TRN2", target_bir_lowering=False)

    def declare(self):
        nc, cfg = self.nc, self.cfg
        NT = cfg.NTOK
        I = lambda name, shape: nc.dram_tensor(name, list(shape), F32, kind="ExternalInput").ap()
        self.xT = I("xT", [D, NT])
        self.ctxT = I("ctxT", [D, CTX])
        self.cT = I("cT", [128, NK * 2])
        self.ropeC = I("ropeC", [128, NT])
        self.ropeS = I("ropeS", [128, NT])
        self.kbias = I("kbias", [128, cfg.nblk])
        self.vmask = I("vmask", [128, NT])
        self.consts = I("consts", [128, 384])
        self.W = {}
        for i in cfg.layers:
            w = {}
            w["mod"] = I(f"wmod{i}", [48, 128, SLOT])
            w["bm"] = I(f"bm{i}", [128, 96])
            w["ng"] = I(f"ng{i}", [128, 64])
            if i % 2 == 0:
                w["win"] = I(f"win{i}", [16, 128, 6144])
                w["ck"] = I(f"ck{i}", [128, 48])
                w["wout"] = I(f"wout{i}", [8, 128, 4096])
            else:
                w["wq"] = I(f"wq{i}", [16, 128, 4096])
                w["wk"] = I(f"wk{i}", [4, 128, 4096])
                w["wv"] = I(f"wv{i}", [2, 128, 4096])
                w["wout"] = I(f"wo{i}", [8, 128, 4096])
                w["sk"] = I(f"sk{i}", [128, 16])
            w["wgu"] = I(f"wgu{i}", [NF, 128, 4096])
            w["wd"] = I(f"wd{i}", [2 * NK, 128, 22 * 128])
            self.W[i] = w
        if cfg.dbg:
            self.out = nc.dram_tensor("out", [D, NT], F32, kind="ExternalOutput").ap()
            self.out_off = 0
        else:
            self.out = nc.dram_tensor("out", [D, cfg.per], F32, kind="ExternalOutput").ap()
            self.out_off = HALO
        self.XS = [nc.dram_tensor(f"xs{i}", [D, NT], F32).ap() for i in range(2)]
        self.CS = [nc.dram_tensor(f"cs{i}", [D, CTX], F32).ap() for i in range(2)]

    def ps(self):
        pool = self.ps_pool
        i = pool[self.ps_i % len(pool)]
        self.ps_i += 1
        return self.PS[i], ("PS", i)

    def fs(self):
        i = self.f_i % len(self.FS)
        self.f_i += 1
        return self.FS[i], ("FS", i)

    def bs(self):
        i = self.b_i % len(self.BSC)
        self.b_i += 1
        return self.BSC[i], ("BS", i)

    def wload(self, src, nelem):
        i = self.w_i % NSLOT
        self.w_i += 1
        buf = self.WR[i]
        key = ("WR", i)
        self.tr.dma("pool", f"wr{i}", lambda e: e.dma_start(out=buf[:, 0:nelem], in_=src), writes=[key])
        return buf, key

    class WStream:
        def __init__(self, bld, units, depth=NSLOT - 1):
            self.b = bld
            self.units = units
            self.pos = 0
            self.loaded = []
            self.depth = depth

        def _top(self, upto):
            while len(self.loaded) < min(upto, len(self.units)):
                src, n = self.units[len(self.loaded)]
                self.loaded.append(self.b.wload(src, n))

        def prime(self):
            self._top(self.depth)

        def next(self):
            self._top(self.pos + 1)
            r = self.loaded[self.pos]
            self.pos += 1
            return r

        def after_use(self):
            self._top(self.pos + self.depth)

    def build(self):
        nc, cfg = self.nc, self.cfg
        self.declare()
        with ExitStack() as st:
            self.st = st
            tr = self.tr = Tracker(nc, st)
            sb = lambda name, shape, dt: st.enter_context(nc.sbuf_tensor(name, list(shape), dt))
            self.X = sb("X", [128, NK, 768], F32)
            self.HYraw = sb("HY", [128, 8192], F32)
            self.H = self.HYraw[:].bitcast(BF16)
            self.A = sb("A", [128, NF, 512], BF16)
            self.WR = [sb(f"WR{i}", [128, SLOT], BF16) for i in range(NSLOT)]
            self.RS = sb("RS", [128, 768], F32)
            self.FS = [sb(f"FS{i}", [128, 512], F32) for i in range(6)]
            self.BSC = [sb(f"BS{i}", [128, 512], BF16) for i in range(4)]
            self.RC = sb("RC", [128, 768], F32)
            self.RSN = sb("RSN", [128, 768], F32)
            self.KTC = sb("KTC", [128, 4, 256], BF16)
            self.VVC = sb("VVC", [128, 2, 512], BF16)
            self.CF = sb("CF", [128, 384], F32)
            self.CB = sb("CB", [128, 384], BF16)
            self.MOD = sb("MOD", [128, 96, 2], F32)
            self.VEC = sb("VEC", [128, 6, 2, NK], F32)
            self.BM = sb("BM", [128, 96], F32)
            self.NG = sb("NG", [128, 64], F32)
            self.CK = sb("CK", [128, 48], F32)
            self.SK = sb("SK", [128, 16], F32)
            self.KB = sb("KB", [128, cfg.nblk], F32)
            self.CT = sb("CT", [128, NK * 2], F32)
            self.CTB = sb("CTB", [128, NK * 2], BF16)
            self.VM = sb("VM", [128, 512], F32)
            self.PS = [st.enter_context(nc.psum_tensor(f"PS{i}", [128, 512], F32)) for i in range(8)]
            self.ps_pool = list(range(7))
            self.ps_i = self.f_i = self.b_i = self.w_i = 0

            tr.dma("sp", "cf", lambda e: e.dma_start(out=self.CF[:], in_=self.consts), writes=["CF"])
            tr.op("dve", lambda e: e.tensor_copy(out=self.CB[:], in_=self.CF[:]), reads=["CF"], writes=["CB"])
            tr.dma("sp", "kb", lambda e: e.dma_start(out=self.KB[:], in_=self.kbias), writes=["KB"])
            tr.dma("sp", "ct", lambda e: e.dma_start(out=self.CT[:], in_=self.cT), writes=["CT"])
            tr.op("act", lambda e: e.activation(out=self.CTB[:], in_=self.CT[:], func=AF.Silu),
                  reads=["CT"], writes=["CTB"])
            self.ONES = self.CB[:, 0:128]
            self.MGE = self.CB[:, 128:256]
            self.MLE = self.CB[:, 256:384]

            src_x, src_c = (self.xT, 0, "xT"), (self.ctxT, "ctxT")
            nl = len(cfg.layers)
            for li, i in enumerate(cfg.layers):
                last = (li == nl - 1)
                if last:
                    dst_x = (self.out, self.out_off, "out")
                else:
                    dst_x = (self.XS[li % 2], 0, f"xs{li % 2}")
                dst_c = (self.CS[li % 2], f"cs{li % 2}")
                self.layer(i, src_x, dst_x, src_c, dst_c, last and i == 3)
                src_x, src_c = dst_x, dst_c

            for name, sem, val in self.last_store:
                nc.sync.wait_ge(sem, val)
        return nc

    def layer(self, i, src_x, dst_x, src_c, dst_c, final):
        cfg, tr = self.cfg, self.tr
        w = self.W[i]
        conv = (i % 2 == 0)
        tr.dma("sp", "bm", lambda e: e.dma_start(out=self.BM[:], in_=w["bm"]), writes=["BM"])
        tr.dma("sp", "ng", lambda e: e.dma_start(out=self.NG[:], in_=w["ng"]), writes=["NG"])
        if conv:
            tr.dma("sp", "ck", lambda e: e.dma_start(out=self.CK[:], in_=w["ck"]), writes=["CK"])
        else:
            tr.dma("sp", "sk", lambda e: e.dma_start(out=self.SK[:], in_=w["sk"]), writes=["SK"])
            tr.op("act", lambda e: e.activation(out=self.SK[:], in_=self.SK[:], func=AF.Exp),
                  reads=["SK"], writes=["SK"])
        self.modulation(i)

        lo, hi = cfg.lrange(i)
        tiles = []
        step = 510 if conv else 512
        t = lo
        while t < hi:
            n = min(step, hi - t)
            tiles.append((False, t, n))
            t += n
        if cfg.tiles is not None:
            tiles = tiles[:cfg.tiles[i]]
        tiles = [(True, 0, CTX)] + tiles
        units = []
        for (is_ctx, _, _) in tiles:
            units += self.tile_units(i, is_ctx, final)
        ws = self.WStream(self, units)
        ws.prime()
        for (is_ctx, t0, n) in tiles:
            self.tile(i, ws, is_ctx, t0, n, src_x, dst_x, src_c, dst_c, final)

    def tile_units(self, i, is_ctx, final):
        w = self.W[i]
        u = []
        if i % 2 == 0:
            for j in range(16):
                u += [(w["win"][j][:, 0:4096], 4096), (w["win"][j][:, 4096:6144], 2048)]
            u += [(w["wout"][l], 4096) for l in range(8)]
        else:
            u += [(w["wk"][g], 4096) for g in range(4)]
            u += [(w["wv"][l], 4096) for l in range(2)]
            if not (is_ctx and final):
                u += [(w["wq"][j], 4096) for j in range(16)]
                u += [(w["wout"][l], 4096) for l in range(8)]
        if not (is_ctx and final):
            u += [(w["wgu"][f], 4096) for f in range(NF)]
            u += [(w["wd"][d], 22 * 128) for d in range(2 * NK)]
        return u

    def modulation(self, i):
        tr = self.tr
        w = self.W[i]
        units = [(w["mod"][l], SLOT) for l in range(48)]
        ws = self.WStream(self, units)
        ws.prime()
        pst, psk = self.ps()
        ctb = self.CTB[:].rearrange("p (k r) -> p k r", r=2)
        for l in range(48):
            buf, key = ws.next()
            wv = buf[:, 0:SLOT].rearrange("p (a k c) -> p a k c", a=2, k=NK)

            def mm(e):
                ins = None
                for a in range(2):
                    ch = 2 * l + a
                    for k in range(NK):
                        ins = e.matmul(pst[:, 2 * ch:2 * ch + 2], wv[:, a, k, :], ctb[:, k, :],
                                       start=(k == 0), stop=(k == NK - 1))
                return ins
            tr.op("pe", mm, reads=[key, "CTB"], writes=[psk])
            ws.after_use()
        modp = pst[:, 0:192].rearrange("p (c r) -> p c r", r=2)
        for r in range(2):
            tr.op("dve", lambda e: e.tensor_tensor(out=self.MOD[:, :, r], in0=modp[:, :, r], in1=self.BM[:], op=ALU.add),
                  reads=[psk, "BM"], writes=["MOD"])
        M = self.MOD
        V = self.VEC
        ng = self.NG[:].rearrange("p (n k) -> p n k", n=4)
        for r in range(2):
            sec = lambda s: M[:, 16 * s:16 * s + 16, r]
            for (dst, scs, gn) in ((0, 1, 0), (3, 4, 2)):
                tr.op("dve", lambda e: e.scalar_tensor_tensor(out=V[:, dst, r, :], in0=sec(scs), scalar=1.0, in1=ng[:, gn, :],
                                                              op0=ALU.add, op1=ALU.mult), reads=["MOD", "NG"], writes=["VEC"])
            for (dst, s) in ((1, 0), (4, 3)):
                tr.op("dve", lambda e: e.tensor_copy(out=V[:, dst, r, :], in_=sec(s)), reads=["MOD"], writes=["VEC"])
            for (dst, s, gn) in ((2, 2, 1), (5, 5, 3)):
                tr.op("dve", lambda e: e.tensor_tensor(out=V[:, dst, r, :], in0=sec(s), in1=ng[:, gn, :], op=ALU.mult),
                      reads=["MOD", "NG"], writes=["VEC"])

    def ss_start(self, n):
        self.ss_n, self.ss_k, self.ss_pend = n, 0, None

    def _ss_flush(self):
        if self.ss_pend is None:
            return
        sq, sqk = self.ss_pend
        k, n = self.ss_k, self.ss_n
        self.tr.op("pe", lambda e: e.matmul(self.PS[7][:, 0:n], self.ONES, sq[:, 0:n], start=(k == 0), stop=(k == NK - 1),
                                            skip_group_check=True), reads=[sqk, "CB"], writes=[("PS", 7)])
        self.ss_k += 1
        self.ss_pend = None

    def ss_push(self, src, key):
        n = self.ss_n
        sq, sqk = self.bs()
        self.tr.op("act", lambda e: e.activation(out=sq[:, 0:n], in_=src, func=AF.Square), reads=[key], writes=[sqk])
        self._ss_flush()
        self.ss_pend = (sq, sqk)

    def ss_finish(self):
        self._ss_flush()
        assert self.ss_k == NK
        tr, n = self.tr, self.ss_n
        rs = self.RS[:, 0:n]
        tr.op("dve", lambda e: e.tensor_scalar(out=rs, in0=self.PS[7][:, 0:n], scalar1=1.0 / D, scalar2=EPS,
                                               op0=ALU.mult, op1=ALU.add), reads=[("PS", 7)], writes=["RS"])
        tr.op("act", lambda e: e.activation(out=rs, in_=rs, func=AF.Sqrt), reads=["RS"], writes=["RS"])
        tr.op("dve", lambda e: e.reciprocal(out=rs, in_=rs), reads=["RS"], writes=["RS"])

    def rstd(self, srcs, keys, n, rs_off):
        tr = self.tr
        c = 0
        while c < n:
            m = min(512, n - c)
            pst, psk = self.ps()
            for k in range(NK):
                sq, sqk = self.bs()
                tr.op("act", lambda e: e.activation(out=sq[:, 0:m], in_=srcs[k][:, c:c + m], func=AF.Square),
                      reads=[keys[k]], writes=[sqk])
                tr.op("pe", lambda e: e.matmul(pst[:, 0:m], self.ONES, sq[:, 0:m], start=(k == 0), stop=(k == NK - 1),
                                               skip_group_check=True),
                      reads=[sqk, "CB"], writes=[psk])
            rs = self.RS[:, rs_off + c:rs_off + c + m]
            tr.op("dve", lambda e: e.tensor_scalar(out=rs, in0=pst[:, 0:m], scalar1=1.0 / D, scalar2=EPS,
                                                   op0=ALU.mult, op1=ALU.add), reads=[psk], writes=["RS"])
            tr.op("act", lambda e: e.activation(out=rs, in_=rs, func=AF.Sqrt), reads=["RS"], writes=["RS"])
            tr.op("dve", lambda e: e.reciprocal(out=rs, in_=rs), reads=["RS"], writes=["RS"])
            c += m

    def prenorm(self, n, xc, which, r):
        tr = self.tr
        srcs = [self.X[:, k, xc:xc + n] for k in range(NK)]
        keys = [("X", k) for k in range(NK)]
        if which == 1:
            self.rstd(srcs, keys, n, 0)
        else:
            self.ss_finish()
        V = self.VEC
        gs, sh = (0, 1) if which == 1 else (3, 4)
        Hv = self.H.rearrange("p (k t) -> p k t", k=NK)
        for k in range(NK):
            c = 0
            while c < n:
                m = min(512, n - c)
                tmp, tk = self.fs()
                tr.op("dve", lambda e: e.tensor_tensor(out=tmp[:, 0:m], in0=srcs[k][:, c:c + m], in1=self.RS[:, c:c + m], op=ALU.mult),
                      reads=[("X", k), "RS"], writes=[tk])
                tr.op("act", lambda e: e.activation(out=Hv[:, k, c:c + m], in_=tmp[:, 0:m], func=AF.Identity,
                                                    scale=V[:, gs, r, k:k + 1], bias=V[:, sh, r, k:k + 1]),
                      reads=[tk, "VEC"], writes=[("HY", k)])
                c += m
        return Hv

    def postnorm(self, n, xc, which, r):
        tr = self.tr
        Yv = self.HYraw[:].rearrange("p (k t) -> p k t", k=NK)
        srcs = [Yv[:, d, 0:n] for d in range(NK)]
        keys = [("HY", d) for d in range(NK)]
        self.ss_finish()
        if which == 1:
            self.ss_start(n)
        V = self.VEC
        gg = 2 if which == 1 else 5
        for d in range(NK):
            tmp, tk = self.fs()
            tr.op("dve", lambda e: e.scalar_tensor_tensor(out=tmp[:, 0:n], in0=srcs[d], scalar=V[:, gg, r, d:d + 1], in1=self.RS[:, 0:n],
                                                          op0=ALU.mult, op1=ALU.mult), reads=[("HY", d), "RS", "VEC"], writes=[tk])
            xs = self.X[:, d, xc:xc + n]
            tr.op("pool" if d % 2 == 0 else "dve", lambda e: e.tensor_tensor(out=xs, in0=xs, in1=tmp[:, 0:n], op=ALU.add),
                  reads=[tk, ("X", d)], writes=[("X", d)])
            if which == 1:
                self.ss_push(xs, ("X", d))

    def h_to_y(self):
        pass

    def y_to_h(self):
        pass

    def out_proj(self, ws, acts, akeys, n):
        tr = self.tr
        Yv = self.HYraw[:].rearrange("p (k t) -> p k t", k=NK)
        self.ss_start(n)
        for l in range(8):
            buf, key = ws.next()
            wv = buf[:, 0:4096].rearrange("p (a j c) -> p a j c", a=2, j=NK)
            for a in range(2):
                d = 2 * l + a
                pst, psk = self.ps()

                def mm(e):
                    ins = None
                    for j in range(NK):
                        ins = e.matmul(pst[:, 0:n], wv[:, a, j, :], acts[j], start=(j == 0), stop=(j == NK - 1))
                    return ins
                tr.op("pe", mm, reads=[key] + akeys, writes=[psk])
                tr.op("act", lambda e: e.copy(out=Yv[:, d, 0:n], in_=pst[:, 0:n]), reads=[psk], writes=[("HY", d)])
                self.ss_push(Yv[:, d, 0:n], ("HY", d))
            ws.after_use()

    def tile(self, i, ws, is_ctx, t0, n, src_x, dst_x, src_c, dst_c, final):
        cfg, tr = self.cfg, self.tr
        conv = (i % 2 == 0)
        r = 1 if is_ctx else 0
        if is_ctx:
            ncol, xc = CTX, 0
            src = src_c[0].rearrange("(k p) t -> p k t", p=128)
            sname = src_c[1]
        elif conv:
            ncol, xc = n + 2, 1
            sa, so, sname = src_x
            src = sa.rearrange("(k p) t -> p k t", p=128)[:, :, t0 - 1 - so:t0 - 1 - so + ncol]
        else:
            ncol, xc = n + 256, 128
            sa, so, sname = src_x
            src = sa.rearrange("(k p) t -> p k t", p=128)[:, :, t0 - 128 - so:t0 - 128 - so + ncol]
        xkeys = [("X", k) for k in range(NK)]
        for k in range(NK):
            tr.dma("sp", f"xi{k}", lambda e: e.dma_start(out=self.X[:, k, 0:ncol], in_=src[:, k, :]),
                   reads=[("DR", sname, k)], writes=[("X", k)])

        Hv = self.prenorm(ncol, 0, 1, r)
        if conv:
            self.conv_mixer(i, ws, is_ctx, t0, n, ncol, Hv)
        else:
            self.attn_mixer(i, ws, is_ctx, t0, n, ncol, Hv, final)
        if is_ctx and final:
            return
        self.postnorm(n, xc, 1, r)
        self.y_to_h()

        Hv = self.prenorm(n, xc, 2, r)
        self.ffn(ws, n, Hv)
        self.postnorm(n, xc, 2, r)
        self.y_to_h()

        if is_ctx:
            dst = dst_c[0].rearrange("(k p) t -> p k t", p=128)
            dname = dst_c[1]
        else:
            da, do, dname = dst_x
            dst = da.rearrange("(k p) t -> p k t", p=128)[:, :, t0 - do:t0 - do + n]
        self.last_store = [tr.dma("sp", f"xo{k}", lambda e: e.dma_start(out=dst[:, k, :], in_=self.X[:, k, xc:xc + n]),
                                  reads=[("X", k)], writes=[("DR", dname, k)]) for k in range(NK)]

    def ffn(self, ws, n, Hv):
        tr = self.tr
        hkeys = [("HY", k) for k in range(NK)]
        for f in range(NF):
            buf, key = ws.next()
            wv = buf[:, 0:4096].rearrange("p (a k c) -> p a k c", a=2, k=NK)
            pg, pgk = self.ps()
            pu, puk = self.ps()

            def mm(e, dst, a):
                ins = None
                for k in range(NK):
                    ins = e.matmul(dst[:, 0:n], wv[:, a, k, :], Hv[:, k, 0:n], start=(k == 0), stop=(k == NK - 1))
                return ins
            tr.op("pe", lambda e: mm(e, pg, 0), reads=[key] + hkeys, writes=[pgk])
            tr.op("pe", lambda e: mm(e, pu, 1), reads=[key] + hkeys, writes=[puk])
            sg, sgk = self.fs()
            tr.op("act", lambda e: e.activation(out=sg[:, 0:n], in_=pg[:, 0:n], func=AF.Silu), reads=[pgk], writes=[sgk])
            tr.op("dve", lambda e: e.tensor_tensor(out=self.A[:, f, 0:n], in0=pu[:, 0:n], in1=sg[:, 0:n], op=ALU.mult),
                  reads=[puk, sgk], writes=[("A", f)])
            ws.after_use()
        self.h_to_y()
        Yv = self.HYraw[:].rearrange("p (k t) -> p k t", k=NK)
        akeys = [("A", f) for f in range(NF)]
        self.ss_start(n)
        for d in range(NK):
            pst, psk = self.ps()
            for hf in range(2):
                buf, key = ws.next()
                wv = buf[:, 0:22 * 128].rearrange("p (f c) -> p f c", f=22)

                def mmd(e):
                    ins = None
                    for f in range(22):
                        ff = 22 * hf + f
                        ins = e.matmul(pst[:, 0:n], wv[:, f, :], self.A[:, ff, 0:n], start=(ff == 0), stop=(ff == NF - 1),
                                       skip_group_check=True)
                    return ins
                tr.op("pe", mmd, reads=[key] + akeys, writes=[psk])
                ws.after_use()
            tr.op("act", lambda e: e.copy(out=Yv[:, d, 0:n], in_=pst[:, 0:n]), reads=[psk], writes=[("HY", d)])
            self.ss_push(Yv[:, d, 0:n], ("HY", d))

    def conv_mixer(self, i, ws, is_ctx, t0, n, ncol, Hv):
        tr = self.tr
        hkeys = [("HY", k) for k in range(NK)]
        ck = self.CK[:].rearrange("p (j t) -> p j t", t=3)
        boundary = False
        if not is_ctx:
            lo, hi = self.cfg.lrange(i)
            boundary = (t0 - 1 < HALO) or (t0 + n + 1 > self.cfg.NTOK - HALO)
            if boundary:
                tr.dma("sp", "vm", lambda e: e.dma_start(out=self.VM[:, 0:ncol], in_=self.vmask[:, t0 - 1:t0 - 1 + ncol]), writes=["VM"])
        for j in range(NK):
            buf, key = ws.next()
            wvA = buf[:, 0:4096].rearrange("p (a k c) -> p a k c", a=2, k=NK)
            buf2, key2 = ws.next()
            wvB = buf2[:, 0:2048].rearrange("p (a k c) -> p a k c", a=1, k=NK)
            pss = [self.ps() for _ in range(3)]

            def mm(e, dst, wv, a):
                ins = None
                for k in range(NK):
                    ins = e.matmul(dst[:, 0:ncol], wv[:, a, k, :], Hv[:, k, 0:ncol], start=(k == 0), stop=(k == NK - 1))
                return ins
            tr.op("pe", lambda e: mm(e, pss[0][0], wvA, 0), reads=[key] + hkeys, writes=[pss[0][1]])
            tr.op("pe", lambda e: mm(e, pss[1][0], wvA, 1), reads=[key] + hkeys, writes=[pss[1][1]])
            tr.op("pe", lambda e: mm(e, pss[2][0], wvB, 0), reads=[key2] + hkeys, writes=[pss[2][1]])
            (pb, pbk), (pc, pck), (pv, pvk) = pss
            cs, csk = self.fs()
            tr.op("act", lambda e: e.copy(out=cs[:, 0:ncol], in_=pc[:, 0:ncol]), reads=[pck], writes=[csk])
            u, uk = self.fs()
            uo = 1 if is_ctx else 0
            if is_ctx:
                tr.op("pool", lambda e: e.memset(u[:, 0:ncol + 2], 0.0), writes=[uk])
            tr.op("dve", lambda e: e.tensor_tensor(out=u[:, uo:uo + ncol], in0=pv[:, 0:ncol], in1=cs[:, 0:ncol], op=ALU.mult),
                  reads=[pvk, csk], writes=[uk])
            if boundary:
                tr.op("pool", lambda e: e.tensor_tensor(out=u[:, 0:ncol], in0=u[:, 0:ncol], in1=self.VM[:, 0:ncol], op=ALU.mult),
                      reads=[uk, "VM"], writes=[uk])
            cv, cvk = self.fs()
            tr.op("act", lambda e: e.activation(out=cv[:, 0:n], in_=u[:, 0:n], func=AF.Identity, scale=ck[:, j, 0:1]),
                  reads=[uk, "CK"], writes=[cvk])
            for tap in (1, 2):
                tr.op("dve", lambda e: e.scalar_tensor_tensor(out=cv[:, 0:n], in0=u[:, tap:tap + n], scalar=ck[:, j, tap:tap + 1],
                                                               in1=cv[:, 0:n], op0=ALU.mult, op1=ALU.add),
                      reads=[uk, cvk, "CK"], writes=[cvk])
            bo = 0 if is_ctx else 1
            tr.op("dve", lambda e: e.tensor_tensor(out=self.A[:, j, 0:n], in0=pb[:, bo:bo + n], in1=cv[:, 0:n], op=ALU.mult),
                  reads=[pbk, cvk], writes=[("A", j)])
            ws.after_use()
        self.h_to_y()
        self.out_proj(ws, [self.A[:, j, 0:n] for j in range(NK)], [("A", j) for j in range(NK)], n)

    def attn_mixer(self, i, ws, is_ctx, t0, n, ncol, Hv, final):
        tr, cfg = self.tr, self.cfg
        hkeys = [("HY", k) for k in range(NK)]
        A = self.A
        nkb = ncol // 128
        nqb = n // 128
        qoff = 0 if is_ctx else 128
        KT = A[:, 32:38, :].rearrange("p s t -> p (s t)").rearrange("p (g t) -> p g t", g=4)
        VV = A[:, 38:44, :]
        akq = [("A", s) for s in range(0, 16)]
        ako = [("A", s) for s in range(16, 32)]
        akk = [("A", s) for s in range(32, 38)]
        akv = [("A", s) for s in range(38, 44)]
        if is_ctx:
            KTt, VVt, kkeys, vkeys = self.KTC, self.VVC, ["KTC"], ["VVC"]
        else:
            KTt, VVt, kkeys, vkeys = KT, VV, akk, akv
            tr.dma("sp", "rp", lambda e: [e.dma_start(out=self.RC[:, 0:ncol], in_=self.ropeC[:, t0 - 128:t0 - 128 + ncol]),
                                           e.dma_start(out=self.RSN[:, 0:ncol], in_=self.ropeS[:, t0 - 128:t0 - 128 + ncol])],
                   writes=["ROPE"])

        def proj2(buf, key, cols):
            wv = buf[:, 0:4096].rearrange("p (a k c) -> p a k c", a=2, k=NK)
            outs = []
            for a in range(2):
                pst, psk = self.ps()

                def mm(e):
                    ins = None
                    for k in range(NK):
                        ins = e.matmul(pst[:, 0:cols[1] - cols[0]], wv[:, a, k, :], Hv[:, k, cols[0]:cols[1]],
                                       start=(k == 0), stop=(k == NK - 1))
                    return ins
                tr.op("pe", mm, reads=[key] + hkeys, writes=[psk])
                outs.append((pst, psk))
                if is_ctx:
                    break
            return outs

        def rope(outs, cols, dst, dkey):
            m = cols[1] - cols[0]
            if is_ctx:
                (p0, k0), = outs
                tr.op("act", lambda e: e.copy(out=dst, in_=p0[:, 0:m]), reads=[k0], writes=dkey)
                return
            (p0, k0), (p1, k1) = outs
            t1, t1k = self.fs()
            t2, t2k = self.fs()
            tr.op("dve", lambda e: e.tensor_tensor(out=t1[:, 0:m], in0=p0[:, 0:m], in1=self.RC[:, cols[0]:cols[1]], op=ALU.mult),
                  reads=[k0, "ROPE"], writes=[t1k])
            tr.op("dve", lambda e: e.tensor_tensor(out=t2[:, 0:m], in0=p1[:, 0:m], in1=self.RSN[:, cols[0]:cols[1]], op=ALU.mult),
                  reads=[k1, "ROPE"], writes=[t2k])
            tr.op("pool", lambda e: e.tensor_tensor(out=dst, in0=t1[:, 0:m], in1=t2[:, 0:m], op=ALU.add),
                  reads=[t1k, t2k], writes=dkey)

        for g in range(4):
            buf, key = ws.next()
            c = 0
            while c < ncol:
                m = min(512, ncol - c)
                outs = proj2(buf, key, (c, c + m))
                rope(outs, (c, c + m), KTt[:, g, c:c + m], ["KTC"] if is_ctx else
                     [("A", 32 + q) for q in range((g * 768 + c) // 512, (g * 768 + c + m - 1) // 512 + 1)])
                c += m
            ws.after_use()
        for l in range(2):
            buf, key = ws.next()
            wv = buf[:, 0:4096].rearrange("p (k c) -> p k c", k=NK)
            for kb in range(nkb):
                pst, psk = self.ps()

                def mm(e):
                    ins = None
                    for k in range(NK):
                        ins = e.matmul(pst[:, 0:256], Hv[:, k, kb * 128:(kb + 1) * 128], wv[:, k, :],
                                       start=(k == 0), stop=(k == NK - 1))
                    return ins
                tr.op("pe", mm, reads=[key] + hkeys, writes=[psk])
                tr.op("act", lambda e: e.copy(out=VVt[:, kb, l * 256:(l + 1) * 256], in_=pst[:, 0:256]),
                      reads=[psk], writes=["VVC"] if is_ctx else [("A", 38 + kb)])
            ws.after_use()
        if is_ctx and final:
            return
        for j in range(NK):
            buf, key = ws.next()
            outs = proj2(buf, key, (qoff, qoff + n))
            rope(outs, (qoff, qoff + n), A[:, j, 0:n], [("A", j)])
            ws.after_use()

        self.ps_pool = [0, 1, 2, 3]
        items = []
        for h in range(32):
            j, half = h // 2, h % 2
            g = h // 8
            rows = slice(64 * half, 64 * half + 64)
            chunks = []
            for c in range(2):
                chunks.append((self.KTC[rows, g, c * 128:(c + 1) * 128], self.VVC[:, c, g * 128:(g + 1) * 128],
                               (0, n), None, [], ["KTC", "VVC"]))
            if not is_ctx:
                blk0 = (t0 - 1) // 128 - 1
                for kb in range(nkb):
                    q0, q1 = max(0, kb - 2), min(nqb - 1, kb)
                    masks = []
                    if kb - 2 >= 0 and kb - 2 <= nqb - 1:
                        masks.append(((kb - 2 - q0) * 128, self.MLE))
                    if kb <= nqb - 1:
                        masks.append(((kb - q0) * 128, self.MGE))
                    chunks.append((KT[rows, g, kb * 128:(kb + 1) * 128], VV[:, kb, g * 128:(g + 1) * 128],
                                   (q0 * 128, (q1 + 1) * 128), self.KB[:, blk0 + kb:blk0 + kb + 1], masks, akk + akv))
            for ci, ch in enumerate(chunks):
                items.append((h, j, rows, ci, len(chunks), ch))

        def front(it):
            h, j, rows, ci, nch, (kap, vap, (c0, c1), bias, masks, rk) = it
            m = c1 - c0
            pst, psk = self.ps()
            tr.op("pe", lambda e: e.matmul(pst[:, 0:m], kap, A[rows, j, c0:c1], start=True, stop=True),
                  reads=rk + [("A", j)], writes=[psk])
            pt, ptk = self.bs()
            if bias is None:
                tr.op("act", lambda e: e.activation(out=pt[:, 0:m], in_=pst[:, 0:m], func=AF.Exp, scale=0.125),
                      reads=[psk], writes=[ptk])
            else:
                tr.op("act", lambda e: e.activation(out=pt[:, 0:m], in_=pst[:, 0:m], func=AF.Exp, scale=0.125, bias=bias),
                      reads=[psk, "KB"], writes=[ptk])
            for (mo, mk) in masks:
                tr.op("pool", lambda e: e.tensor_tensor(out=pt[:, mo:mo + 128], in0=pt[:, mo:mo + 128], in1=mk, op=ALU.mult),
                      reads=[ptk, "CB"], writes=[ptk])
            return (it, pt, ptk)

        def back(ent):
            (h, j, rows, ci, nch, (kap, vap, (c0, c1), bias, masks, rk)), pt, ptk = ent
            m = c1 - c0
            po, pok = self.PS[4 + (h % 2)], ("PS", 4 + (h % 2))
            pd, pdk = self.PS[6 + (h % 2)], ("PS", 6 + (h % 2))
            first = (ci == 0)
            lastc = (ci == nch - 1)
            tr.op("pe", lambda e: e.matmul(po[:, c0:c1], vap, pt[:, 0:m], start=first, stop=lastc, skip_group_check=True),
                  reads=[ptk] + rk, writes=[pok])
            tr.op("pe", lambda e: e.matmul(pd[:, c0:c1], self.ONES, pt[:, 0:m], start=first, stop=lastc, skip_group_check=True),
                  reads=[ptk, "CB"], writes=[pdk])
            if lastc:
                rd, rdk = self.fs()
                tr.op("dve", lambda e: e.tensor_scalar(out=rd[rows, 0:n], in0=pd[rows, 0:n], scalar1=self.SK[rows, j:j + 1], scalar2=None, op0=ALU.add),
                      reads=[pdk, "SK"], writes=[rdk])
                tr.op("dve", lambda e: e.reciprocal(out=rd[rows, 0:n], in_=rd[rows, 0:n]), reads=[rdk], writes=[rdk])
                tr.op("dve", lambda e: e.tensor_tensor(out=A[rows, 16 + j, 0:n], in0=po[rows, 0:n], in1=rd[rows, 0:n], op=ALU.mult),
                      reads=[pok, rdk], writes=[("A", 16 + j)])

        LA = 2
        pend = []
        for idx in range(len(items) + LA):
            if idx < len(items):
                pend.append(front(items[idx]))
            if idx >= LA:
                back(pend.pop(0))
        self.ps_pool = list(range(7))
        self.h_to_y()
        self.out_proj(ws, [A[:, 16 + j, 0:n] for j in range(NK)], ako, n)


def _partner():
    d = np.arange(64)
    within = d % 32
    return np.where(within < 16, d + 16, d - 16)


def _unit_kc(wcols):
    return wcols.reshape(NK, 128, -1).transpose(1, 0, 2)


def prep_weights(inp, layers):
    out = {}
    f = lambda a: np.ascontiguousarray(a, dtype=np.float32)
    part = _partner()
    for i in layers:
        wm = np.asarray(inp["w_mod"][i])
        out[f"wmod{i}"] = f(wm.reshape(NK, 128, 48, 2, 128).transpose(2, 1, 3, 0, 4)).reshape(48, 128, SLOT)
        out[f"bm{i}"] = f(np.asarray(inp["b_mod"][i]).reshape(96, 128).T)
        out[f"ng{i}"] = f(np.asarray(inp["norm_g"][i]).reshape(4, NK, 128).transpose(2, 0, 1)).reshape(128, 64)
        j = i // 2
        if i % 2 == 0:
            wi = np.asarray(inp["conv_w_in"][j])
            out[f"win{i}"] = f(wi.reshape(NK, 128, 3, NK, 128).transpose(3, 1, 2, 0, 4)).reshape(16, 128, 6144)
            out[f"ck{i}"] = f(np.asarray(inp["conv_k"][j]).reshape(3, NK, 128).transpose(2, 1, 0)).reshape(128, 48)
            wo = np.asarray(inp["conv_w_out"][j])
            out[f"wout{i}"] = f(wo.reshape(NK, 128, 8, 2, 128).transpose(2, 1, 3, 0, 4)).reshape(8, 128, 4096)
        else:
            wqkv = np.asarray(inp["attn_w_qkv"][j])
            wq = wqkv[:, :2048]
            colperm = (np.arange(2048) // 64) * 64 + part[np.arange(2048) % 64]
            wq2 = np.stack([wq, wq[:, colperm]], axis=0)
            out[f"wq{i}"] = f(wq2.reshape(2, NK, 128, NK, 128).transpose(3, 2, 0, 1, 4)).reshape(16, 128, 4096)
            wk = wqkv[:, 2048:2304].reshape(2048, 4, 64)
            wkd = np.concatenate([wk, wk], axis=2)
            wkp = wk[:, :, part]
            wkpd = np.concatenate([wkp, wkp], axis=2)
            wk2 = np.stack([wkd, wkpd], axis=0)
            out[f"wk{i}"] = f(wk2.reshape(2, NK, 128, 4, 128).transpose(3, 2, 0, 1, 4)).reshape(4, 128, 4096)
            wv = wqkv[:, 2304:2560].reshape(2048, 4, 64)
            wvd = np.concatenate([wv, wv], axis=2).reshape(2048, 2, 256)
            out[f"wv{i}"] = f(wvd.reshape(NK, 128, 2, 256).transpose(2, 1, 0, 3)).reshape(2, 128, 4096)
            wo = np.asarray(inp["attn_w_o"][j])
            out[f"wo{i}"] = f(wo.reshape(NK, 128, 8, 2, 128).transpose(2, 1, 3, 0, 4)).reshape(8, 128, 4096)
            sk = np.asarray(inp["attn_sink"][j])
            out[f"sk{i}"] = f(sk[2 * np.arange(16)[None, :] + (np.arange(128)[:, None] // 64)])
        wg = np.asarray(inp["ffn_w_gate"][i]).reshape(NK, 128, NF, 128).transpose(2, 1, 0, 3)
        wu = np.asarray(inp["ffn_w_up"][i]).reshape(NK, 128, NF, 128).transpose(2, 1, 0, 3)
        out[f"wgu{i}"] = f(np.stack([wg, wu], axis=2)).reshape(NF, 128, 4096)
        out[f"wd{i}"] = f(np.asarray(inp["ffn_w_down"][i]).reshape(2, 22, 128, NK, 128).transpose(3, 0, 2, 1, 4)).reshape(2 * NK, 128, 22 * 128)
    return out


def core_inputs(inp, cfg, c):
    cpb = cfg.NC // B
    b, s = c // cpb, c % cpb
    lo = s * cfg.per - HALO
    NT = cfg.NTOK
    gt = lo + np.arange(NT)
    valid = (gt >= 0) & (gt < S)
    x = np.asarray(inp["x"][b])
    xT = np.zeros((D, NT), np.float32)
    xT[:, valid] = x[gt[valid]].T
    m = {"xT": xT, "ctxT": np.ascontiguousarray(np.asarray(inp["ctx"][b]).T)}
    cc = np.stack([np.asarray(inp["c"][b]), np.asarray(inp["c_ctx"])], axis=-1)
    m["cT"] = np.ascontiguousarray(cc.reshape(NK, 128, 2).transpose(1, 0, 2)).reshape(128, NK * 2).astype(np.float32)
    p = np.arange(128)
    d = p % 64
    fidx = d % 16
    inv = (np.float32(10000.0) ** (-np.arange(16, dtype=np.float32) / np.float32(16))).astype(np.float32)
    gtc = np.clip(gt, 0, S - 1)
    pos = np.where((d // 32)[:, None] == 0, (gtc // 64)[None, :], (gtc % 64)[None, :]).astype(np.float32)
    ang = (pos * inv[fidx][:, None]).astype(np.float32)
    sign = np.where((d % 32) < 16, -1.0, 1.0).astype(np.float32)[:, None]
    m["ropeC"] = np.cos(ang).astype(np.float32)
    m["ropeS"] = (np.sin(ang).astype(np.float32) * sign).astype(np.float32)
    kb = np.zeros((128, cfg.nblk), np.float32)
    tl = 1 + 128 * np.arange(cfg.nblk)[None, :] + np.arange(128)[:, None]
    kb[~valid[tl]] = NEGB
    m["kbias"] = kb
    m["vmask"] = np.broadcast_to(valid.astype(np.float32)[None, :], (128, NT)).copy()
    kk = np.arange(128)[:, None]
    qq = np.arange(128)[None, :]
    m["consts"] = np.concatenate([np.ones((128, 128)), (kk >= qq), (kk <= qq)], axis=1).astype(np.float32)
    return m


_CFG = Cfg(nc_cores=8)


def kernel(**inputs):
    cfg = _CFG
    bld = Builder(cfg)
    nc = bld.build()
    wts = prep_weights(inputs, cfg.layers)
    in_maps = []
    for c in range(cfg.NC):
        m = core_inputs(inputs, cfg, c)
        m.update(wts)
        in_maps.append(m)
    res = run_bass_kernel_spmd(nc, in_maps, core_ids=list(range(cfg.NC)))
    cpb = cfg.NC // B
    out = np.empty((B, S, D), np.float32)
    for c in range(cfg.NC):
        b, s = c // cpb, c % cpb
        out[b, s * cfg.per:(s + 1) * cfg.per, :] = res.results[c]["out"].T
    return out
```
